# Optimizing a Trainium2 kernel written in Bass

```python
import math
import jax, jax.numpy as jnp
from jax import lax
import numpy as np

D_MODEL = 1024
BATCH = 8
SEQ = 4096
DEPTH = 4

MIX_WIDTH = D_MODEL
FOX_WIDTH = MIX_WIDTH // 2
FOX_HEAD_DIM = 64
FOX_HEADS = FOX_WIDTH // FOX_HEAD_DIM
POOL_WIDTH = MIX_WIDTH - FOX_WIDTH
POOL_WINDOWS = (2, 4, 8, 16)
POOL_GROUPS = len(POOL_WINDOWS)
POOL_GROUP_DIM = POOL_WIDTH // POOL_GROUPS
Q_BLOCK = 128
MEM_LEN = 256
X_HEADS = 4
X_HEAD_DIM = D_MODEL // X_HEADS
D_FF = 4 * D_MODEL
EPS = 1e-6
IN_COLS = 3 * FOX_WIDTH + FOX_HEADS + POOL_WIDTH

kernel_name = "fox_pool_hybrid_memory_trunk"


def rms_norm(x, g):
    x32 = x.astype(jnp.float32)
    y = x32 * lax.rsqrt(jnp.mean(x32 * x32, axis=-1, keepdims=True) + EPS)
    return (y * g.astype(jnp.float32)).astype(x.dtype)


def forgetting_attention(q, k, v, fg_logit):
    S = q.shape[1]
    scale = 1.0 / math.sqrt(q.shape[-1])
    log_f = jax.nn.log_sigmoid(fg_logit.astype(jnp.float32))
    c = jnp.transpose(jnp.cumsum(log_f, axis=1), (0, 2, 1))
    outs = []
    for blk in range(S // Q_BLOCK):
        q0, q1 = blk * Q_BLOCK, (blk + 1) * Q_BLOCK
        qb = q[:, q0:q1]
        kb = k[:, :q1]
        vb = v[:, :q1]
        s = jnp.einsum('bqhd,bkhd->bhqk', qb, kb).astype(jnp.float32) * scale
        s = s + c[:, :, q0:q1, None] - c[:, :, None, :q1]
        causal = jnp.arange(q1)[None, :] <= jnp.arange(q0, q1)[:, None]
        s = jnp.where(causal, s, -jnp.inf)
        p = jax.nn.softmax(s, axis=-1).astype(v.dtype)
        outs.append(jnp.einsum('bhqk,bkhd->bqhd', p, vb))
    return jnp.concatenate(outs, axis=1)


def causal_pool_mixer(u, w_groups, scale):
    B, S, _ = u.shape
    ug = u.reshape(B, S, POOL_GROUPS, POOL_GROUP_DIM)
    u32 = ug.astype(jnp.float32)
    csum = jnp.cumsum(u32, axis=1)
    pos = jnp.arange(S)
    pooled = []
    for g, w in enumerate(POOL_WINDOWS):
        cg = csum[:, :, g]
        lag = jnp.concatenate([jnp.zeros((B, w, POOL_GROUP_DIM), jnp.float32), cg[:, :S - w]], axis=1)
        count = jnp.minimum(pos + 1, w).astype(jnp.float32)[None, :, None]
        pooled.append((cg - lag) / count - u32[:, :, g])
    pooled = jnp.stack(pooled, axis=2).astype(u.dtype)
    y = jnp.einsum('bsgc,gcd->bsgd', pooled, w_groups)
    return y.reshape(B, S, POOL_WIDTH) * scale


def memory_cross_attention(h, m, wq, wkv, wo):
    B, S, _ = h.shape
    q = (h @ wq).reshape(B, S, X_HEADS, X_HEAD_DIM)
    kv = m @ wkv
    k = kv[..., :D_MODEL].reshape(B, MEM_LEN, X_HEADS, X_HEAD_DIM)
    v = kv[..., D_MODEL:].reshape(B, MEM_LEN, X_HEADS, X_HEAD_DIM)
    s = jnp.einsum('bqhd,bkhd->bhqk', q, k).astype(jnp.float32) / math.sqrt(X_HEAD_DIM)
    p = jax.nn.softmax(s, axis=-1).astype(v.dtype)
    o = jnp.einsum('bhqk,bkhd->bqhd', p, v).reshape(B, S, D_MODEL)
    return o @ wo


def setup_inputs(seed: int = 0) -> dict:
    key = jax.random.key(seed)
    ks = jax.random.split(key, 20)
    f32 = jnp.float32

    def w(k, shape, fan_in):
        return jax.random.normal(k, shape, f32) * (fan_in ** -0.5)

    def gain(k):
        return 1.0 + 0.02 * jax.random.normal(k, (DEPTH, D_MODEL), f32)

    return {
        "x": jax.random.normal(ks[0], (BATCH, SEQ, D_MODEL), f32),
        "mem": jax.random.normal(ks[1], (BATCH, MEM_LEN, D_MODEL), f32),
        "g_mix_pre": gain(ks[2]),
        "w_in": w(ks[3], (DEPTH, D_MODEL, IN_COLS), D_MODEL),
        "b_forget": 2.0 + 0.1 * jax.random.normal(ks[4], (DEPTH, FOX_HEADS), f32),
        "pool_w": w(ks[5], (DEPTH, POOL_GROUPS, POOL_GROUP_DIM, POOL_GROUP_DIM), POOL_GROUP_DIM),
        "pool_scale": 1.0 + 0.02 * jax.random.normal(ks[6], (DEPTH, POOL_WIDTH), f32),
        "w_out": w(ks[7], (DEPTH, MIX_WIDTH, D_MODEL), MIX_WIDTH),
        "g_mix_post": gain(ks[8]),
        "g_x_pre": gain(ks[9]),
        "g_mem": gain(ks[10]),
        "wq_x": w(ks[11], (DEPTH, D_MODEL, D_MODEL), D_MODEL),
        "wkv_x": w(ks[12], (DEPTH, D_MODEL, 2 * D_MODEL), D_MODEL),
        "wo_x": w(ks[13], (DEPTH, D_MODEL, D_MODEL), D_MODEL),
        "g_x_post": gain(ks[14]),
        "g_ffn_pre": gain(ks[15]),
        "w_up": w(ks[16], (DEPTH, D_MODEL, D_FF), D_MODEL),
        "w_down": w(ks[17], (DEPTH, D_FF, D_MODEL), D_FF),
        "g_ffn_post": gain(ks[18]),
    }


def reference(x, mem, g_mix_pre, w_in, b_forget, pool_w, pool_scale, w_out, g_mix_post,
              g_x_pre, g_mem, wq_x, wkv_x, wo_x, g_x_post, g_ffn_pre, w_up, w_down, g_ffn_post):
    B, S, _ = x.shape
    o_q, o_k, o_v = 0, FOX_WIDTH, 2 * FOX_WIDTH
    o_f = 3 * FOX_WIDTH
    o_p = o_f + FOX_HEADS
    for l in range(DEPTH):
        h = rms_norm(x, g_mix_pre[l])
        proj = h @ w_in[l]
        q = proj[..., o_q:o_k].reshape(B, S, FOX_HEADS, FOX_HEAD_DIM)
        k = proj[..., o_k:o_v].reshape(B, S, FOX_HEADS, FOX_HEAD_DIM)
        v = proj[..., o_v:o_f].reshape(B, S, FOX_HEADS, FOX_HEAD_DIM)
        fg_logit = proj[..., o_f:o_p] + b_forget[l]
        u = proj[..., o_p:]
        attn_out = forgetting_attention(q, k, v, fg_logit).reshape(B, S, FOX_WIDTH)
        pool_out = causal_pool_mixer(u, pool_w[l], pool_scale[l])
        mix = jnp.concatenate([attn_out, pool_out], axis=-1) @ w_out[l]
        x = x + rms_norm(mix, g_mix_post[l])
        h = rms_norm(x, g_x_pre[l])
        m = rms_norm(mem, g_mem[l])
        xo = memory_cross_attention(h, m, wq_x[l], wkv_x[l], wo_x[l])
        x = x + rms_norm(xo, g_x_post[l])
        h = rms_norm(x, g_ffn_pre[l])
        a = jnp.square(jax.nn.relu(h @ w_up[l]))
        x = x + rms_norm(a @ w_down[l], g_ffn_post[l])
    return x
```

```python
import numpy as np
import concourse.bass as bass
import concourse.mybir as mybir
from concourse.bass_utils import run_bass_kernel_spmd

F32 = mybir.dt.float32
BF16 = mybir.dt.bfloat16
U8 = mybir.dt.uint8
AF = mybir.ActivationFunctionType
ALU = mybir.AluOpType

D = 1024
S = 4096
DEPTH = 4
NT = S // 128
NCH = S // 512
DFF = 4096
MEM = 256
EPS = 1e-6
IN_COLS = 2056
N_CORES = 8
NCONST = 592 + 128 + 1024


class Op:
    __slots__ = ("eng", "fn", "deps", "marked", "count", "dsem", "idx", "isdma")


class Sched:
    ENGS = ("pe", "act", "dve", "pool", "sp")

    def __init__(self, nc):
        self.nc = nc
        self.ops = []
        self.last_w = {}
        self.readers = {}
        self.dsem_counts = {}
        self.fence_deps = {e: [] for e in self.ENGS}
        self.capturing = None

    def capture(self, fn):
        assert self.capturing is None
        self.capturing = []
        fn()
        lst, self.capturing = self.capturing, None
        return lst

    def replay(self, lst, n=None):
        n = len(lst) if n is None else min(n, len(lst))
        for _ in range(n):
            self.add(*lst.pop(0))

    def add(self, eng, fn, reads=(), writes=(), dsem=None):
        if self.capturing is not None:
            self.capturing.append((eng, fn, tuple(reads), tuple(writes), dsem))
            return None
        op = Op()
        op.eng = eng
        op.fn = fn
        op.dsem = dsem
        op.isdma = dsem is not None
        op.idx = len(self.ops)
        op.marked = False
        op.count = 0
        deps = list(self.fence_deps[eng])
        self.fence_deps[eng] = []
        for r in reads:
            w = self.last_w.get(r)
            if w is not None:
                deps.append(w)
        for r in writes:
            w = self.last_w.get(r)
            if w is not None:
                deps.append(w)
            deps.extend(self.readers.get(r, ()))
        for r in reads:
            self.readers.setdefault(r, []).append(op)
        for r in writes:
            self.last_w[r] = op
            self.readers[r] = []
        best = {}
        for d in deps:
            if d is op:
                continue
            key = ("d", d.dsem) if d.isdma else ("e", d.eng)
            if key not in best or best[key].idx < d.idx:
                best[key] = d
        if eng == "pe" and not op.isdma:
            best.pop(("e", "pe"), None)
        op.deps = list(best.values())
        for d in op.deps:
            d.marked = True
        if op.isdma:
            self.dsem_counts[dsem] = self.dsem_counts.get(dsem, 0) + 16
            op.count = self.dsem_counts[dsem]
        self.ops.append(op)
        return op

    def fence(self):
        last = {}
        for op in self.ops:
            key = ("d", op.dsem) if op.isdma else ("e", op.eng)
            last[key] = op
        for e in self.ENGS:
            self.fence_deps[e] = list(last.values())

    def emit(self, final_dsems=()):
        nc = self.nc
        cnt = {e: 0 for e in self.ENGS}
        for op in self.ops:
            if not op.isdma and op.marked:
                cnt[op.eng] += 1
                op.count = cnt[op.eng]
        esem = {e: nc.alloc_semaphore("s_" + e) for e in self.ENGS if e != "sp"}
        dsem = {k: nc.alloc_semaphore("d_%d" % i) for i, k in enumerate(self.dsem_counts)}
        per_eng = {e: [o for o in self.ops if o.eng == e] for e in self.ENGS}

        def run(e, handle):
            waited = {}
            for op in per_eng[e]:
                for d in op.deps:
                    sem = dsem[d.dsem] if d.isdma else esem[d.eng]
                    if waited.get(sem, 0) < d.count:
                        handle.wait_ge(sem, d.count)
                        waited[sem] = d.count
                ins = op.fn(handle)
                if op.isdma:
                    ins.then_inc(dsem[op.dsem], 16)
                elif op.marked:
                    ins.then_inc(esem[e], 1)
            if e == "sp":
                for k in final_dsems:
                    handle.wait_ge(dsem[k], self.dsem_counts[k])

        with nc.Block() as block:
            @block.sync
            def _(h):
                run("sp", h)

            @block.tensor
            def _(h):
                run("pe", h)

            @block.vector
            def _(h):
                run("dve", h)

            @block.scalar
            def _(h):
                run("act", h)

            @block.gpsimd
            def _(h):
                run("pool", h)


class Arena:
    def __init__(self, nc, nbytes):
        self.t = nc.alloc_sbuf_tensor("arena", [128, nbytes], U8)
        self.nbytes = nbytes
        self.off = 0

    def mark(self):
        return self.off

    def reset(self, off):
        self.off = off

    def alloc(self, shape, dtype):
        esz = 4 if dtype == F32 else 2
        n = 1
        for s in shape:
            n *= s
        nb = (n * esz + 63) // 64 * 64
        assert self.off + nb <= self.nbytes, ("SBUF arena overflow", self.off + nb, self.nbytes)
        ap = self.t[:, self.off:self.off + n * esz].bitcast(dtype)
        self.off += nb
        if len(shape) == 2:
            ap = ap.rearrange("p (a b) -> p a b", a=shape[0])
        elif len(shape) == 3:
            ap = ap.rearrange("p (a b c) -> p a b c", a=shape[0], b=shape[1])
        return ap


class K:
    pass


def rms_stats(k, src_ap, src_res, slot, junk, junk_res):
    sc = k.sc
    ss = k.stat[:, slot, 0:1]
    ln = k.stat[:, slot, 1:2]
    rs = k.stat[:, slot, 2:3]
    rn = "stat%d" % slot
    sc.add("act", lambda e: e.activation(out=junk, in_=src_ap, func=AF.Square, accum_out=ss),
           reads=(list(src_res) if isinstance(src_res, list) else [src_res]), writes=[rn, junk_res])
    sc.add("act", lambda e: e.activation(out=ln, in_=ss, func=AF.Ln, scale=1.0 / D, bias=k.epsb),
           reads=[rn], writes=[rn])
    sc.add("act", lambda e: e.activation(out=rs, in_=ln, func=AF.Exp, scale=-0.5),
           reads=[rn], writes=[rn])
    return rs, rn


def load_bcast(k, dst, src_row, res, dsem):
    k.sc.add("sp", lambda e: e.dma_start(out=dst, in_=src_row.partition_broadcast(128)),
             writes=[res], dsem=dsem)


def norm_tile(k, i, gpre, gres, src=None, src_res=None):
    sc = k.sc
    s2 = i % 2
    xin = k.xin[s2]
    xr = "xin%d" % s2
    if src is None:
        src = k.xs[i * 128:(i + 1) * 128, :]
        src_res = ("xs", i)
    sc.add("sp", lambda e: e.dma_start(out=xin, in_=src),
           reads=[src_res], writes=[xr], dsem=xr)
    sh = i % len(k.hb)
    hb = k.hb[sh]
    hr = "hb%d" % sh
    rs, rn = rms_stats(k, xin, xr, s2, hb, hr)
    sc.add("dve", lambda e: e.scalar_tensor_tensor(out=hb, in0=xin, scalar=rs, in1=gpre,
                                                   op0=ALU.mult, op1=ALU.mult),
           reads=[xr, rn, gres], writes=[hr])
    return hb, hr


def transpose_tile(k, i, hb, hr, hT, hT_res, tcol, psT_banks):
    sc = k.sc
    b = psT_banks[i % len(psT_banks)]
    psT = k.bankT(b)
    pr = "bank%d" % b
    for kc in range(8):
        sc.add("pe", lambda e, kc=kc: e.transpose(psT[:, kc, :], hb[:, kc * 128:(kc + 1) * 128], k.ident),
               reads=[hr, "ident"], writes=[pr])
    sc.add("dve", lambda e: e.tensor_copy(out=hT[:, :, tcol:tcol + 128], in_=psT),
           reads=[pr], writes=[hT_res])


def norm_transpose_tile(k, i, gpre, gres, hT, hT_res, tcol, psT_banks, src=None, src_res=None):
    hb, hr = norm_tile(k, i, gpre, gres, src, src_res)
    transpose_tile(k, i, hb, hr, hT, hT_res, tcol, psT_banks)


def post_norm_residual(k, i, psD, psD_res, gpost, gres):
    sc = k.sc
    s2 = i % 2
    yt = k.yt[s2]
    yr = "yt%d" % s2
    rs, rn = rms_stats(k, psD, psD_res, 2 + s2, yt.bitcast(BF16)[:, 0:D], yr)
    sc.add("dve", lambda e: e.scalar_tensor_tensor(out=yt, in0=psD, scalar=rs, in1=gpost,
                                                   op0=ALU.mult, op1=ALU.mult),
           reads=list(psD_res) + [rn, gres], writes=[yr])
    sc.add("pool", lambda e: e.dma_start(out=k.xs[i * 128:(i + 1) * 128, :], in_=yt, accum_op=ALU.add),
           reads=[yr], writes=[("xs", i)], dsem=yr)


def load_w(k, dst, src2d, res, nsplit=8):
    per = 8 // nsplit
    for j in range(nsplit):
        k.sc.add("pool", lambda e, j=j: e.dma_start(
            out=dst[:, j * per:(j + 1) * per, :],
            in_=src2d[j * per * 128:(j + 1) * per * 128, :].rearrange("(c p) n -> p c n", p=128)),
            writes=[res], dsem=res)


def cross_phase(k, l):
    sc = k.sc
    ar = k.arena
    m0 = ar.mark()
    wq = ar.alloc([8, D], BF16)
    wo = ar.alloc([8, D], BF16)
    wkv = ar.alloc([8, 2 * D], BF16)
    gpre = ar.alloc([D], F32)
    gpost = ar.alloc([D], F32)
    gmem = ar.alloc([D], F32)
    mT = ar.alloc([8, MEM], BF16)
    kxT = ar.alloc([8, MEM], BF16)
    vx = ar.alloc([2, D], BF16)
    hT = ar.alloc([8, 512], BF16)
    qT = ar.alloc([8, 512], BF16)
    oT = ar.alloc([8, 512], BF16)
    pT = [ar.alloc([512], BF16) for _ in range(2)]
    rec = ar.alloc([512], F32)
    load_w(k, wkv, k.wkv_x[l], "wkv")
    load_w(k, wq, k.wq_x[l], "wq")
    load_w(k, wo, k.wo_x[l], "wo")
    load_bcast(k, gmem, k.g_mem[l:l + 1, :], "gmem", "gmem")
    load_bcast(k, gpre, k.g_x_pre[l:l + 1, :], "gpre", "gpre")
    load_bcast(k, gpost, k.g_x_post[l:l + 1, :], "gpost", "gpost")
    for mt in range(2):
        norm_transpose_tile(k, mt, gmem, "gmem", mT, "mT", mt * 128, (0,),
                            src=k.mem[mt * 128:(mt + 1) * 128, :], src_res="mem")
    for m in range(8):
        b = 1 + m % 2
        ps = k.bank(b)[:, 0:MEM]
        for kc in range(8):
            sc.add("pe", lambda e, kc=kc, m=m, ps=ps: e.matmul(
                ps, wkv[:, kc, m * 128:(m + 1) * 128], mT[:, kc, :], start=(kc == 0), stop=(kc == 7)),
                reads=["wkv", "mT"], writes=["bank%d" % b])
        sc.add("dve", lambda e, m=m, ps=ps: e.tensor_copy(out=kxT[:, m, :], in_=ps),
               reads=["bank%d" % b], writes=["kxT"])
    for mt in range(2):
        for half in range(2):
            b = 1 + half
            ps = k.bank(b)
            for kc in range(8):
                sc.add("pe", lambda e, kc=kc, mt=mt, half=half, ps=ps: e.matmul(
                    ps, mT[:, kc, mt * 128:(mt + 1) * 128], wkv[:, kc, D + half * 512:D + (half + 1) * 512],
                    start=(kc == 0), stop=(kc == 7)),
                    reads=["wkv", "mT"], writes=["bank%d" % b])
            sc.add("act", lambda e, mt=mt, half=half, ps=ps: e.activation(
                out=vx[:, mt, half * 512:(half + 1) * 512], in_=ps, func=AF.Copy),
                reads=["bank%d" % b], writes=["vx"])

    def normT(c):
        for t in range(4):
            norm_transpose_tile(k, 4 * c + t, gpre, "gpre", hT, "hT", t * 128, (0,))

    def attn(c):
        for m in range(8):
            b = 1 + m % 2
            ps = k.bank(b)
            for kc in range(8):
                sc.add("pe", lambda e, kc=kc, m=m, ps=ps: e.matmul(
                    ps, wq[:, kc, m * 128:(m + 1) * 128], hT[:, kc, :], start=(kc == 0), stop=(kc == 7)),
                    reads=["wq", "hT"], writes=["bank%d" % b])
            sc.add("act", lambda e, m=m, ps=ps: e.activation(out=qT[:, m, :], in_=ps, func=AF.Copy, scale=1.0 / 16.0),
                   reads=["bank%d" % b], writes=["qT"])
        for h in range(4):
            for mt in range(2):
                b = 1 + mt
                ps = k.bank(b)
                for j in range(2):
                    sc.add("pe", lambda e, j=j, mt=mt, h=h, ps=ps: e.matmul(
                        ps, kxT[:, 2 * h + j, mt * 128:(mt + 1) * 128], qT[:, 2 * h + j, :],
                        start=(j == 0), stop=(j == 1)),
                        reads=["kxT", "qT"], writes=["bank%d" % b])
                sc.add("act", lambda e, mt=mt, ps=ps: e.activation(out=pT[mt], in_=ps, func=AF.Exp),
                       reads=["bank%d" % b], writes=["pT%d" % mt])
            for mt in range(2):
                sc.add("pe", lambda e, mt=mt: e.matmul(k.bank(3), k.ones, pT[mt], start=(mt == 0), stop=(mt == 1)),
                       reads=["ones", "pT%d" % mt], writes=["bank3"])
            for dj in range(2):
                for mt in range(2):
                    sc.add("pe", lambda e, mt=mt, dj=dj, h=h: e.matmul(
                        k.bank(4 + dj), vx[:, mt, h * 256 + dj * 128:h * 256 + (dj + 1) * 128], pT[mt],
                        start=(mt == 0), stop=(mt == 1)),
                        reads=["vx", "pT%d" % mt], writes=["bank%d" % (4 + dj)])
            sc.add("dve", lambda e: e.reciprocal(out=rec, in_=k.bank(3)), reads=["bank3"], writes=["rec"])
            for dj in range(2):
                sc.add("dve", lambda e, dj=dj, h=h: e.tensor_tensor(
                    out=oT[:, 2 * h + dj, :], in0=k.bank(4 + dj), in1=rec, op=ALU.mult),
                    reads=["bank%d" % (4 + dj), "rec"], writes=["oT"])

    def oproj(c):
        for t in range(4):
            i = 4 * c + t
            psD = k.bank2(6)
            for half in range(2):
                for kc in range(8):
                    sc.add("pe", lambda e, kc=kc, half=half, t=t, psD=psD: e.matmul(
                        psD[:, half * 512:(half + 1) * 512], oT[:, kc, t * 128:(t + 1) * 128],
                        wo[:, kc, half * 512:(half + 1) * 512], start=(kc == 0), stop=(kc == 7)),
                        reads=["oT", "wo"], writes=["bank%d" % (6 + half)])
            post_norm_residual(k, i, psD, ["bank6", "bank7"], gpost, "gpost")

    normT(0)
    for c in range(NCH):
        attn(c)
        if c + 1 < NCH:
            normT(c + 1)
        oproj(c)
    sc.fence()
    ar.reset(m0)


def mix_phase(k, l):
    sc = k.sc
    ar = k.arena
    m0 = ar.mark()
    hb_save = k.hb
    k.hb = k.hb + [ar.alloc([D], BF16) for _ in range(2)]
    win = ar.alloc([8, IN_COLS], BF16)
    wout = ar.alloc([8, D], BF16)
    poolw = ar.alloc([4, 128], BF16)
    pscale = ar.alloc([4], F32)
    gpre = ar.alloc([D], F32)
    gpost = ar.alloc([D], F32)
    bfb4 = ar.alloc([4, 8], F32)
    kT2 = ar.alloc([4, S], BF16)
    vaug = ar.alloc([NT, 8, 65], BF16)
    k.cf = ar.alloc([NCONST - 128 - 1024], F32)
    k.identf = k.cf[:, 0:128]
    k.negtri = k.cf[:, 128:256]
    k.oneslast = k.cf[:, 256:384]
    k.maskneg = k.cf[:, 384:512]
    k.invc = k.cf[:, 512:528]
    k.onesf = k.cf[:, 528:592]
    k.sel = ar.alloc([8, 128], BF16)
    sc.add("dve", lambda e: e.memset(vaug[:, :, :, 64:65], 1.0), writes=["vaug"])
    sc.add("sp", lambda e: e.dma_start(out=k.cf, in_=k.consts[:, 0:NCONST - 1024 - 128]), writes=["cf"], dsem="cf")
    sc.add("dve", lambda e: e.memset(k.sel, 0.0), writes=["cf"])
    sc.add("pool", lambda e: e.dma_start(
        out=k.sel[0:16], in_=k.consts[0:16, NCONST - 1024:NCONST].rearrange("p (h n) -> p h n", h=8)),
        writes=["cf"], dsem="sel")
    c_all = ar.alloc([NT, 8], F32)
    biasG = ar.alloc([NT, 8], F32)
    crefbc = [ar.alloc([8], F32) for _ in range(2)]
    hT = ar.alloc([8, 512], BF16)
    qT2 = ar.alloc([4, 512], BF16)
    dq = ar.alloc([512], BF16)
    catT = ar.alloc([8, 512], BF16)
    uT = ar.alloc([4, 528], F32)
    pA = ar.alloc([528], F32)
    pB = ar.alloc([528], F32)
    tmpc = ar.alloc([16], F32)
    poolT = ar.alloc([4, 512], BF16)
    pT = [ar.alloc([512], BF16) for _ in range(4)]
    oun = [ar.alloc([512], F32) for _ in range(2)]
    zt = ar.alloc([32], F32)
    e1 = ar.alloc([32], F32)
    sp = ar.alloc([32], F32)
    Wsb = ar.alloc([4, 8], F32)
    R = ar.alloc([5, 8], F32)
    crel = ar.alloc([4, 8], F32)
    hib = ar.alloc([4, 8], BF16)
    X = ar.alloc([4, 16], F32)

    load_w(k, win, k.w_in[l], "win")
    load_w(k, wout, k.w_out[l], "wout", nsplit=4)
    sc.add("pool", lambda e: e.dma_start(out=poolw, in_=k.pool_w[l].rearrange("g c d -> c g d")),
           writes=["poolw"], dsem="poolw")
    sc.add("sp", lambda e: e.dma_start(out=pscale, in_=k.pool_scale_t[l]), writes=["pscale"], dsem="pscale")
    load_bcast(k, gpre, k.g_mix_pre[l:l + 1, :], "gpre", "gpre")
    load_bcast(k, gpost, k.g_mix_post[l:l + 1, :], "gpost", "gpost")
    for t in range(4):
        sc.add("sp", lambda e, t=t: e.dma_start(out=bfb4[:, t, :], in_=k.b_forget[l:l + 1, :].partition_broadcast(128)),
               writes=["bfb4"], dsem="bfb4")
    sc.add("dve", lambda e: e.memset(uT[:, :, 0:16], 0.0), writes=["uT"])
    sc.add("dve", lambda e: e.memset(dq, 0.0), writes=["dq"])
    sc.add("dve", lambda e: e.memset(crefbc[0], 0.0), writes=["cref0"])
    sc.add("dve", lambda e: e.memset(R[:, 0, :], 0.0), writes=["R"])

    normed = {}

    def normA(c):
        for t in range(4):
            normed[4 * c + t] = norm_tile(k, 4 * c + t, gpre, "gpre")

    def normB(c):
        for t in range(4):
            hb, hr = normed.pop(4 * c + t)
            transpose_tile(k, 4 * c + t, hb, hr, hT, "hT", t * 128, (0,))

    rot = [0]

    def pbank():
        rot[0] ^= 1
        return 1 + rot[0]

    def proj_fm(col0, evac):
        b = pbank()
        ps = k.bank(b)
        for kc in range(8):
            sc.add("pe", lambda e, kc=kc, ps=ps: e.matmul(ps, win[:, kc, col0:col0 + 128], hT[:, kc, :],
                                                          start=(kc == 0), stop=(kc == 7)),
                   reads=["win", "hT"], writes=["bank%d" % b])
        evac(ps, "bank%d" % b)

    def cchain_a(G):
        for t in range(4):
            for kc in range(8):
                sc.add("pe", lambda e, kc=kc, t=t: e.matmul(
                    k.bank(7)[:, t * 8:(t + 1) * 8], hT[:, kc, t * 128:(t + 1) * 128], win[:, kc, 1536:1544],
                    start=(kc == 0), stop=(kc == 7)),
                    reads=["win", "hT"], writes=["b7fg"])
        sc.add("dve", lambda e: e.tensor_tensor(out=zt, in0=k.bank(7)[:, 0:32], in1=bfb4.rearrange("p a b -> p (a b)"),
                                                op=ALU.add),
               reads=["b7fg", "bfb4"], writes=["zt"])
        sc.add("act", lambda e: e.activation(out=e1, in_=zt, func=AF.Exp, scale=-1.0), reads=["zt"], writes=["e1"])
        sc.add("act", lambda e: e.activation(out=sp, in_=e1, func=AF.Ln, bias=k.onecol), reads=["e1", "onecol"],
               writes=["sp"])

    def cchain_a2(G):
        sc.add("pe", lambda e: e.matmul(k.bank(7)[:, 32:64], k.negtri, sp, start=True, stop=True),
               reads=["sp", "cf"], writes=["b7w"])
        sc.add("dve", lambda e: e.tensor_copy(out=Wsb.rearrange("p a b -> p (a b)"), in_=k.bank(7)[:, 32:64]),
               reads=["b7w"], writes=["Wsb"])
        sc.add("dve", lambda e: e.tensor_copy(out=R[:, 1, :], in_=Wsb[:, 0, :]), reads=["Wsb"], writes=["R"])
        for t in range(2, 5):
            sc.add("dve", lambda e, t=t: e.tensor_tensor(out=R[:, t, :], in0=R[:, t - 1, :], in1=Wsb[:, t - 1, :],
                                                         op=ALU.add),
                   reads=["Wsb", "R"], writes=["R"])

    def cchain_b(G):
        cur = crefbc[G % 2]
        nxt = crefbc[(G + 1) % 2]
        cr, nr = "cref%d" % (G % 2), "cref%d" % ((G + 1) % 2)
        sc.add("pe", lambda e: e.matmul(k.bank(7)[:, 64:104], k.oneslast, R.rearrange("p a b -> p (a b)"),
                                        start=True, stop=True),
               reads=["R", "cf"], writes=["b7r"])
        psR = k.bank(7)[:, 64:104].rearrange("p (a b) -> p a b", a=5)
        sc.add("dve", lambda e: e.tensor_tensor(out=crel, in0=Wsb, in1=psR[:, 0:4, :], op=ALU.add),
               reads=["Wsb", "b7r"], writes=["crel"])
        sc.add("dve", lambda e: e.tensor_tensor(out=c_all[:, 4 * G:4 * G + 4, :], in0=crel,
                                                in1=cur.unsqueeze(1).to_broadcast([128, 4, 8]), op=ALU.add),
               reads=["crel", cr], writes=["c_all"])
        nk = 4 * G + 4
        sc.add("dve", lambda e: e.scalar_tensor_tensor(out=biasG[:, 0:nk, :], in0=c_all[:, 0:nk, :], scalar=-1.0,
                                                       in1=cur.unsqueeze(1).to_broadcast([128, nk, 8]),
                                                       op0=ALU.mult, op1=ALU.add),
               reads=["c_all", cr], writes=["biasG"])
        sc.add("dve", lambda e: e.tensor_tensor(out=nxt, in0=cur, in1=psR[:, 4, :], op=ALU.add),
               reads=[cr, "b7r"], writes=[nr])
        sc.add("dve", lambda e: e.tensor_copy(out=hib, in_=crel), reads=["crel"], writes=["hib"])
        sc.add("dve", lambda e: e.tensor_copy(out=X[:, :, 0:8], in_=hib), reads=["hib"], writes=["X"])
        sc.add("dve", lambda e: e.tensor_tensor(out=X[:, :, 8:16], in0=crel, in1=X[:, :, 0:8], op=ALU.subtract),
               reads=["crel", "X"], writes=["X"])

    def cchain_b2(G):
        b = pbank()
        for t in range(4):
            sc.add("pe", lambda e, t=t, b=b: e.transpose(k.bank(b)[0:16, t * 128:(t + 1) * 128], X[:, t, :], k.identf),
                   reads=["X", "cf"], writes=["bank%d" % b])
        sc.add("dve", lambda e, b=b: e.tensor_copy(out=dq[0:16, :], in_=k.bank(b)[0:16, :]),
               reads=["bank%d" % b], writes=["dq"])


    def projections(G):
        cchain_a(G)
        for p in range(4):
            proj_fm(p * 128, lambda ps, br, p=p: sc.add(
                "act", lambda e: e.activation(out=qT2[:, p, :], in_=ps, func=AF.Copy, scale=0.125),
                reads=[br], writes=["qT2"]))
        cchain_a2(G)
        for p in range(4):
            proj_fm(512 + p * 128, lambda ps, br, p=p: sc.add(
                "dve", lambda e: e.tensor_copy(out=kT2[:, p, G * 512:(G + 1) * 512], in_=ps),
                reads=[br], writes=["kT2"]))
        cchain_b(G)
        for t in range(4):
            b = pbank()
            ps = k.bank(b)
            for kc in range(8):
                sc.add("pe", lambda e, kc=kc, t=t, ps=ps: e.matmul(
                    ps, hT[:, kc, t * 128:(t + 1) * 128], win[:, kc, 1024:1536], start=(kc == 0), stop=(kc == 7)),
                    reads=["win", "hT"], writes=["bank%d" % b])
            sc.add("act", lambda e, t=t, ps=ps: e.activation(
                out=vaug[:, 4 * G + t, :, 0:64], in_=ps.rearrange("p (h d) -> p h d", h=8), func=AF.Copy),
                reads=["bank%d" % b], writes=["vaug"])
        cchain_b2(G)
        if G > 0:
            sc.add("dve", lambda e: e.tensor_copy(out=uT[:, :, 0:16], in_=uT[:, :, 512:528]),
                   reads=["uT"], writes=["uT"])
        for g in range(4):
            proj_fm(1544 + g * 128, lambda ps, br, g=g: sc.add(
                "act", lambda e: e.activation(out=uT[:, g, 16:528], in_=ps, func=AF.Copy),
                reads=[br], writes=["uT"]))

    def pool_mixer(G):
        for g in range(4):
            src = uT[:, g, :]
            L = g + 1
            bufs = [pA, pB]
            names = ["pA", "pB"]
            prev, prev_r = src, "uT"
            lo = 0
            for lev in range(L):
                sh = 1 << lev
                lo2 = lo + sh
                dst, dst_r = bufs[lev % 2], names[lev % 2]
                sc.add("dve", lambda e, dst=dst, prev=prev, lo2=lo2, sh=sh: e.tensor_tensor(
                    out=dst[:, lo2:528], in0=prev[:, lo2:528], in1=prev[:, lo2 - sh:528 - sh], op=ALU.add),
                    reads=[prev_r], writes=[dst_r])
                prev, prev_r, lo = dst, dst_r, lo2
            w = 1 << L
            sc.add("dve", lambda e, g=g, prev=prev, src=src, w=w: e.scalar_tensor_tensor(
                out=poolT[:, g, :], in0=prev[:, 16:528], scalar=1.0 / w, in1=src[:, 16:528],
                op0=ALU.mult, op1=ALU.subtract),
                reads=[prev_r, "uT"], writes=["poolT"])
            if G == 0:
                sc.add("dve", lambda e, prev=prev, w=w: e.tensor_tensor(
                    out=tmpc[:, 0:w - 1], in0=prev[:, 16:16 + w - 1], in1=k.invc[:, 0:w - 1], op=ALU.mult),
                    reads=[prev_r, "cf"], writes=["tmpc"])
                sc.add("dve", lambda e, g=g, src=src, w=w: e.tensor_tensor(
                    out=poolT[:, g, 0:w - 1], in0=tmpc[:, 0:w - 1], in1=src[:, 16:16 + w - 1], op=ALU.subtract),
                    reads=["tmpc", "uT"], writes=["poolT"])
            b = 0
            ps = k.bank(b)
            sc.add("pe", lambda e, g=g, ps=ps: e.matmul(ps, poolw[:, g, :], poolT[:, g, :], start=True, stop=True),
                   reads=["poolw", "poolT"], writes=["bank%d" % b])
            sc.add("act", lambda e, g=g, ps=ps: e.activation(out=catT[:, 4 + g, :], in_=ps, func=AF.Copy,
                                                            scale=pscale[:, g:g + 1]),
                   reads=["bank%d" % b, "pscale"], writes=["catT"])

    def attention(G, bg):
        nkt = 4 * G + 4
        per_it = -(-len(bg) // max(1, 4 * nkt - 3))
        items = [(p, kt) for p in range(4) for kt in range(nkt)]

        def geom(idx):
            p, kt = items[idx]
            j = kt - 4 * G
            sbs = (1 + 2 * (idx % 2), 2 + 2 * (idx % 2))
            pts = (2 * (idx % 2), 2 * (idx % 2) + 1)
            return p, kt, j, 128 * max(j, 0), sbs, pts

        def stageA(idx):
            p, kt, j, c0, sbs, pts = geom(idx)
            for hb_ in range(2):
                pb = 64 * hb_
                psS = k.bank(sbs[hb_])
                sc.add("pe", lambda e, pb=pb, psS=psS: e.matmul(
                    psS[:, c0:512], kT2[pb:pb + 64, p, kt * 128:(kt + 1) * 128], qT2[pb:pb + 64, p, c0:512],
                    start=True, stop=False, tile_position=(pb, 0)),
                    reads=["kT2", "qT2"], writes=["bank%d" % sbs[hb_]])
            for hb_ in range(2):
                h = 2 * p + hb_
                psS = k.bank(sbs[hb_])
                sc.add("pe", lambda e, h=h, psS=psS: e.matmul(
                    psS[:, c0:512], k.sel[:, h, :], dq[:, c0:512], start=False, stop=True),
                    reads=["cf", "dq"], writes=["bank%d" % sbs[hb_]])
            if j >= 0:
                for hb_ in range(2):
                    psS = k.bank(sbs[hb_])
                    sc.add("dve", lambda e, psS=psS: e.tensor_tensor(
                        out=psS[:, c0:c0 + 128], in0=psS[:, c0:c0 + 128], in1=k.maskneg, op=ALU.add),
                        reads=["bank%d" % sbs[hb_], "cf"], writes=["bank%d" % sbs[hb_]])

        def stageB(idx):
            p, kt, j, c0, sbs, pts = geom(idx)
            for hb_ in range(2):
                h = 2 * p + hb_
                psS = k.bank(sbs[hb_])
                pt = pT[pts[hb_]]
                sc.add("act", lambda e, h=h, psS=psS, pt=pt: e.activation(
                    out=pt[:, c0:512], in_=psS[:, c0:512], func=AF.Exp, bias=biasG[:, kt, h:h + 1]),
                    reads=["bank%d" % sbs[hb_], "biasG"], writes=["pT%d" % pts[hb_]])

        def stageC(idx):
            p, kt, j, c0, sbs, pts = geom(idx)
            for hb_ in range(2):
                h = 2 * p + hb_
                psO = k.bank(5 + hb_)
                pt = pT[pts[hb_]]
                sc.add("pe", lambda e, h=h, psO=psO, pt=pt: e.matmul(
                    psO[0:65, c0:512], vaug[:, kt, h, :], pt[:, c0:512], start=(kt == 0), stop=(kt == nkt - 1)),
                    reads=["vaug", "pT%d" % pts[hb_]], writes=["bank%d" % (5 + hb_)])

        def epi1(p):
            for hb_ in range(2):
                psO = k.bank(5 + hb_)
                sc.add("dve", lambda e, hb_=hb_, psO=psO: e.tensor_copy(out=oun[hb_][0:65, :], in_=psO[0:65, :]),
                       reads=["bank%d" % (5 + hb_)], writes=["oun%d" % hb_])
                sc.add("dve", lambda e, hb_=hb_: e.reciprocal(out=oun[hb_][64:65, :], in_=oun[hb_][64:65, :]),
                       reads=["oun%d" % hb_], writes=["oun%d" % hb_])

        def epi2(p):
            for hb_ in range(2):
                pb = 64 * hb_
                sc.add("pe", lambda e, hb_=hb_: e.matmul(k.bank(7)[0:64, :], k.onesf[64:65, 0:64],
                                                          oun[hb_][64:65, :], start=True, stop=True),
                       reads=["oun%d" % hb_, "cf"], writes=["b7fg", "b7w", "b7r"])
                sc.add("dve", lambda e, hb_=hb_, pb=pb: e.tensor_tensor(
                    out=catT[pb:pb + 64, p, :], in0=oun[hb_][0:64, :], in1=k.bank(7)[0:64, :], op=ALU.mult),
                    reads=["oun%d" % hb_, "b7fg", "b7w", "b7r"], writes=["catT"])

        pending = []
        stageA(0)
        for idx in range(len(items)):
            if idx + 1 < len(items):
                stageA(idx + 1)
            stageB(idx)
            stageC(idx)
            sc.replay(bg, per_it)
            for ep in pending:
                ep[1] -= 1
            while pending and pending[0][1] <= 0:
                epi2(pending.pop(0)[0])
            p, kt = items[idx]
            if kt == nkt - 1:
                epi1(p)
                pending.append([p, 3])
        while pending:
            epi2(pending.pop(0)[0])
        sc.replay(bg)

    def outproj(G):
        for t in range(4):
            i = 4 * G + t
            b0 = 1 + 2 * (t % 2)
            psD = k.bank2(b0)
            for half in range(2):
                for kc in range(8):
                    sc.add("pe", lambda e, kc=kc, half=half, t=t, psD=psD: e.matmul(
                        psD[:, half * 512:(half + 1) * 512], catT[:, kc, t * 128:(t + 1) * 128],
                        wout[:, kc, half * 512:(half + 1) * 512], start=(kc == 0), stop=(kc == 7)),
                        reads=["catT", "wout"], writes=["bank%d" % (b0 + half)])
            post_norm_residual(k, i, psD, ["bank%d" % b0, "bank%d" % (b0 + 1)], gpost, "gpost")

    normA(0)
    normB(0)
    for G in range(NCH):
        projections(G)

        def background(G=G):
            pool_mixer(G)
            if G + 1 < NCH:
                normA(G + 1)
                normB(G + 1)
        attention(G, sc.capture(background))
        outproj(G)
    sc.fence()
    k.hb = hb_save
    ar.reset(m0)


def ffn_phase(k, l):
    sc = k.sc
    ar = k.arena
    m0 = ar.mark()
    wup = ar.alloc([8, DFF], BF16)
    wdn = ar.alloc([32, D], BF16)
    gpre = ar.alloc([D], F32)
    gpost = ar.alloc([D], F32)
    hT = ar.alloc([8, 512], BF16)
    aT = ar.alloc([32, 512], BF16)
    r = [ar.alloc([512], BF16) for _ in range(2)]
    for kc in range(8):
        sc.add("pool", lambda e, kc=kc: e.dma_start(out=wup[:, kc, :], in_=k.w_up[l, kc * 128:(kc + 1) * 128, :]),
               writes=["wup"], dsem="wup")
    for j in range(8):
        sc.add("pool", lambda e, j=j: e.dma_start(
            out=wdn[:, 4 * j:4 * j + 4, :],
            in_=k.w_down[l, 512 * j:512 * (j + 1), :].rearrange("(f p) n -> p f n", p=128)),
            writes=["wdn"], dsem="wdn")
    load_bcast(k, gpre, k.g_ffn_pre[l:l + 1, :], "gpre", "gpre")
    load_bcast(k, gpost, k.g_ffn_post[l:l + 1, :], "gpost", "gpost")

    def normT(c):
        for t in range(4):
            norm_transpose_tile(k, 4 * c + t, gpre, "gpre", hT, "hT", t * 128, (0, 1))

    def up(c):
        for f in range(32):
            ps = k.bank(2 + f % 2)
            pr = "bank%d" % (2 + f % 2)
            for kc in range(8):
                sc.add("pe", lambda e, kc=kc, f=f, ps=ps: e.matmul(
                    ps, wup[:, kc, f * 128:(f + 1) * 128], hT[:, kc, :], start=(kc == 0), stop=(kc == 7)),
                    reads=["wup", "hT"], writes=[pr])
            rr = r[f % 2]
            rres = "r%d" % (f % 2)
            sc.add("act", lambda e, ps=ps, rr=rr: e.activation(out=rr, in_=ps, func=AF.Relu),
                   reads=[pr], writes=[rres])
            sc.add("dve", lambda e, f=f, rr=rr: e.tensor_tensor(out=aT[:, f, :], in0=rr, in1=rr, op=ALU.mult),
                   reads=[rres], writes=["aT"])

    def down(c):
        for t in range(4):
            i = 4 * c + t
            b0 = 4 + 2 * (i % 2)
            psD = k.bank2(b0)
            for half in range(2):
                for f in range(32):
                    sc.add("pe", lambda e, f=f, half=half, psD=psD, t=t: e.matmul(
                        psD[:, half * 512:(half + 1) * 512], aT[:, f, t * 128:(t + 1) * 128],
                        wdn[:, f, half * 512:(half + 1) * 512], start=(f == 0), stop=(f == 31)),
                        reads=["aT", "wdn"], writes=["bank%d" % (b0 + half)])
            post_norm_residual(k, i, psD, ["bank%d" % b0, "bank%d" % (b0 + 1)], gpost, "gpost")

    normT(0)
    for c in range(NCH):
        up(c)
        if c + 1 < NCH:
            normT(c + 1)
        down(c)
    sc.fence()
    ar.reset(m0)


def build_program(phases=("mix", "cross", "ffn"), layers=range(DEPTH)):
    nc = bass.Bass("TRN2", target_bir_lowering=False)
    k = K()
    k.nc = nc

    def din(name, shape):
        return nc.dram_tensor(name, list(shape), F32, kind="ExternalInput").ap()

    k.x = din("x", [S, D])
    k.mem = din("mem", [MEM, D])
    k.g_mix_pre = din("g_mix_pre", [DEPTH, D])
    k.w_in = din("w_in", [DEPTH, D, IN_COLS])
    k.b_forget = din("b_forget", [DEPTH, 8])
    k.pool_w = din("pool_w", [DEPTH, 4, 128, 128])
    k.pool_scale = din("pool_scale", [DEPTH, 512])
    k.w_out = din("w_out", [DEPTH, D, D])
    k.g_mix_post = din("g_mix_post", [DEPTH, D])
    k.g_x_pre = din("g_x_pre", [DEPTH, D])
    k.g_mem = din("g_mem", [DEPTH, D])
    k.wq_x = din("wq_x", [DEPTH, D, D])
    k.wkv_x = din("wkv_x", [DEPTH, D, 2 * D])
    k.wo_x = din("wo_x", [DEPTH, D, D])
    k.g_x_post = din("g_x_post", [DEPTH, D])
    k.g_ffn_pre = din("g_ffn_pre", [DEPTH, D])
    k.w_up = din("w_up", [DEPTH, D, DFF])
    k.w_down = din("w_down", [DEPTH, DFF, D])
    k.g_ffn_post = din("g_ffn_post", [DEPTH, D])
    k.consts = din("consts", [128, NCONST])
    k.pool_scale_t = din("pool_scale_t", [DEPTH, 128, 4])
    k.xs = nc.dram_tensor("out", [S, D], F32, kind="ExternalOutput").ap()

    sc = Sched(nc)
    k.sc = sc
    ar = Arena(nc, 206 * 1024)
    k.arena = ar
    k.ident = ar.alloc([128], BF16)
    k.ones = ar.alloc([128], BF16)
    k.onecol = ar.alloc([1], F32)
    k.epsb = ar.alloc([1], F32)
    k.stat = ar.alloc([4, 4], F32)
    k.xin = [ar.alloc([D], F32) for _ in range(2)]
    k.hb = [ar.alloc([D], BF16) for _ in range(2)]
    k.yt = [ar.alloc([D], F32) for _ in range(2)]
    k.ps = nc.alloc_psum_tensor("ps", [128, 8 * 512], F32)
    k.bank = lambda b: k.ps[:, b * 512:(b + 1) * 512]
    k.bank2 = lambda b: k.ps[:, b * 512:(b + 2) * 512]
    k.bankT = lambda b: k.ps[:, b * 512:(b + 1) * 512].bitcast(BF16).rearrange("p (a b) -> p a b", a=8)

    sc.add("pool", lambda e: e.dma_start(out=k.ident, in_=k.consts[:, NCONST - 1024 - 128:NCONST - 1024]),
           writes=["ident"], dsem="ident")
    sc.add("dve", lambda e: e.memset(k.epsb, EPS), writes=["epsb"])
    sc.add("dve", lambda e: e.memset(k.ones, 1.0), writes=["ones"])
    sc.add("dve", lambda e: e.memset(k.onecol, 1.0), writes=["onecol"])

    for c in range(NCH):
        sc.add("sp", lambda e, c=c: e.dma_start(out=k.xs[c * 512:(c + 1) * 512, :], in_=k.x[c * 512:(c + 1) * 512, :]),
               writes=[("xs", 4 * c + t) for t in range(4)], dsem="xcopy%d" % c)
    for l in layers:
        if "mix" in phases:
            mix_phase(k, l)
        if "cross" in phases:
            cross_phase(k, l)
        if "ffn" in phases:
            ffn_phase(k, l)
    final = sorted(set(o.dsem for o in sc.ops if o.isdma), key=str)
    sc.emit(final_dsems=final)
    return nc


INPUT_NAMES = ["x", "mem", "g_mix_pre", "w_in", "b_forget", "pool_w", "pool_scale", "w_out", "g_mix_post",
               "g_x_pre", "g_mem", "wq_x", "wkv_x", "wo_x", "g_x_post", "g_ffn_pre", "w_up", "w_down",
               "g_ffn_post"]


def make_consts():
    c = np.zeros((128, NCONST), np.float32)
    eye = np.eye(128, dtype=np.float32)
    idx = np.arange(128)
    c[:, 0:128] = eye
    c[:, 128:256] = -(idx[:, None] <= idx[None, :]).astype(np.float32)
    c[127, 256:384] = 1.0
    c[:, 384:512] = np.where(idx[:, None] <= idx[None, :], 0.0, -30000.0)
    c[:, 512:528] = 1.0 / (np.arange(16) + 1.0)[None, :]
    c[:, 528:592] = 1.0
    c[:, 592:720] = eye
    selbase = 720
    for h in range(8):
        c[h, selbase + h * 128:selbase + (h + 1) * 128] = 1.0
        c[8 + h, selbase + h * 128:selbase + (h + 1) * 128] = 1.0
    return c


def kernel(**inputs):
    nc = build_program()
    consts = make_consts()
    in_maps = []
    for b in range(N_CORES):
        m = {}
        for n in INPUT_NAMES:
            a = np.asarray(inputs[n], dtype=np.float32)
            if n in ("x", "mem"):
                a = a[b]
            m[n] = np.ascontiguousarray(a)
        m["consts"] = consts
        m["pool_scale_t"] = np.ascontiguousarray(
            np.asarray(inputs["pool_scale"], dtype=np.float32).reshape(DEPTH, 4, 128).transpose(0, 2, 1))
        in_maps.append(m)
    res = run_bass_kernel_spmd(nc, in_maps, core_ids=list(range(N_CORES)))
    return np.stack([np.asarray(r["out"], dtype=np.float32) for r in res.results], axis=0)
```

```python
import numpy as np
import concourse.bass as bass
import concourse.mybir as mybir
from concourse.bass_utils import run_bass_kernel_spmd

F32 = mybir.dt.float32
BF16 = mybir.dt.bfloat16
U8 = mybir.dt.uint8
AF = mybir.ActivationFunctionType
ALU = mybir.AluOpType

D = 1024
S = 4096
DEPTH = 4
NT = S // 128
NCH = S // 512
DFF = 4096
MEM = 256
EPS = 1e-6
IN_COLS = 2056
N_CORES = 8
NCONST = 592 + 128 + 1024


class Op:
    __slots__ = ("eng", "fn", "deps", "marked", "count", "dsem", "idx", "isdma")


class Sched:
    ENGS = ("pe", "act", "dve", "pool", "sp")

    def __init__(self, nc):
        self.nc = nc
        self.ops = []
        self.last_w = {}
        self.readers = {}
        self.dsem_counts = {}
        self.fence_deps = {e: [] for e in self.ENGS}
        self.capturing = None

    def capture(self, fn):
        assert self.capturing is None
        self.capturing = []
        fn()
        lst, self.capturing = self.capturing, None
        return lst

    def replay(self, lst, n=None):
        n = len(lst) if n is None else min(n, len(lst))
        for _ in range(n):
            self.add(*lst.pop(0))

    def add(self, eng, fn, reads=(), writes=(), dsem=None):
        if self.capturing is not None:
            self.capturing.append((eng, fn, tuple(reads), tuple(writes), dsem))
            return None
        op = Op()
        op.eng = eng
        op.fn = fn
        op.dsem = dsem
        op.isdma = dsem is not None
        op.idx = len(self.ops)
        op.marked = False
        op.count = 0
        deps = list(self.fence_deps[eng])
        self.fence_deps[eng] = []
        for r in reads:
            w = self.last_w.get(r)
            if w is not None:
                deps.append(w)
        for r in writes:
            w = self.last_w.get(r)
            if w is not None:
                deps.append(w)
            deps.extend(self.readers.get(r, ()))
        for r in reads:
            self.readers.setdefault(r, []).append(op)
        for r in writes:
            self.last_w[r] = op
            self.readers[r] = []
        best = {}
        for d in deps:
            if d is op:
                continue
            key = ("d", d.dsem) if d.isdma else ("e", d.eng)
            if key not in best or best[key].idx < d.idx:
                best[key] = d
        if eng == "pe" and not op.isdma:
            best.pop(("e", "pe"), None)
        op.deps = list(best.values())
        for d in op.deps:
            d.marked = True
        if op.isdma:
            self.dsem_counts[dsem] = self.dsem_counts.get(dsem, 0) + 16
            op.count = self.dsem_counts[dsem]
        self.ops.append(op)
        return op

    def fence(self):
        last = {}
        for op in self.ops:
            key = ("d", op.dsem) if op.isdma else ("e", op.eng)
            last[key] = op
        for e in self.ENGS:
            self.fence_deps[e] = list(last.values())

    def emit(self, final_dsems=()):
        nc = self.nc
        cnt = {e: 0 for e in self.ENGS}
        for op in self.ops:
            if not op.isdma and op.marked:
                cnt[op.eng] += 1
                op.count = cnt[op.eng]
        esem = {e: nc.alloc_semaphore("s_" + e) for e in self.ENGS if e != "sp"}
        dsem = {k: nc.alloc_semaphore("d_%d" % i) for i, k in enumerate(self.dsem_counts)}
        per_eng = {e: [o for o in self.ops if o.eng == e] for e in self.ENGS}

        def run(e, handle):
            waited = {}
            for op in per_eng[e]:
                for d in op.deps:
                    sem = dsem[d.dsem] if d.isdma else esem[d.eng]
                    if waited.get(sem, 0) < d.count:
                        handle.wait_ge(sem, d.count)
                        waited[sem] = d.count
                ins = op.fn(handle)
                if op.isdma:
                    ins.then_inc(dsem[op.dsem], 16)
                elif op.marked:
                    ins.then_inc(esem[e], 1)
            if e == "sp":
                for k in final_dsems:
                    handle.wait_ge(dsem[k], self.dsem_counts[k])

        with nc.Block() as block:
            @block.sync
            def _(h):
                run("sp", h)

            @block.tensor
            def _(h):
                run("pe", h)

            @block.vector
            def _(h):
                run("dve", h)

            @block.scalar
            def _(h):
                run("act", h)

            @block.gpsimd
            def _(h):
                run("pool", h)


class Arena:
    def __init__(self, nc, nbytes):
        self.t = nc.alloc_sbuf_tensor("arena", [128, nbytes], U8)
        self.nbytes = nbytes
        self.off = 0

    def mark(self):
        return self.off

    def reset(self, off):
        self.off = off

    def alloc(self, shape, dtype):
        esz = 4 if dtype == F32 else 2
        n = 1
        for s in shape:
            n *= s
        nb = (n * esz + 63) // 64 * 64
        assert self.off + nb <= self.nbytes, ("SBUF arena overflow", self.off + nb, self.nbytes)
        ap = self.t[:, self.off:self.off + n * esz].bitcast(dtype)
        self.off += nb
        if len(shape) == 2:
            ap = ap.rearrange("p (a b) -> p a b", a=shape[0])
        elif len(shape) == 3:
            ap = ap.rearrange("p (a b c) -> p a b c", a=shape[0], b=shape[1])
        return ap


class K:
    pass


def rms_stats(k, src_ap, src_res, slot, junk, junk_res):
    sc = k.sc
    ss = k.stat[:, slot, 0:1]
    ln = k.stat[:, slot, 1:2]
    rs = k.stat[:, slot, 2:3]
    rn = "stat%d" % slot
    sc.add("act", lambda e: e.activation(out=junk, in_=src_ap, func=AF.Square, accum_out=ss),
           reads=(list(src_res) if isinstance(src_res, list) else [src_res]), writes=[rn, junk_res])
    sc.add("act", lambda e: e.activation(out=ln, in_=ss, func=AF.Ln, scale=1.0 / D, bias=k.epsb),
           reads=[rn, "epsb"], writes=[rn])
    sc.add("act", lambda e: e.activation(out=rs, in_=ln, func=AF.Exp, scale=-0.5),
           reads=[rn], writes=[rn])
    return rs, rn


def load_bcast(k, dst, src_row, res, dsem):
    k.sc.add("sp", lambda e: e.dma_start(out=dst, in_=src_row.partition_broadcast(128)),
             writes=[res], dsem=dsem)


def norm_tile(k, i, gpre, gres, src=None, src_res=None):
    sc = k.sc
    s2 = i % 2
    xin = k.xin[s2]
    xr = "xin%d" % s2
    if src is None:
        src = k.xs[i * 128:(i + 1) * 128, :]
        src_res = ("xs", i)
    sc.add("sp", lambda e: e.dma_start(out=xin, in_=src),
           reads=[src_res], writes=[xr], dsem=xr)
    sh = i % len(k.hb)
    hb = k.hb[sh]
    hr = "hb%d" % sh
    rs, rn = rms_stats(k, xin, xr, s2, hb, hr)
    sc.add("dve", lambda e: e.scalar_tensor_tensor(out=hb, in0=xin, scalar=rs, in1=gpre,
                                                   op0=ALU.mult, op1=ALU.mult),
           reads=[xr, rn, gres], writes=[hr])
    return hb, hr


def transpose_tile(k, i, hb, hr, hT, hT_res, tcol, psT_banks):
    sc = k.sc
    b = psT_banks[i % len(psT_banks)]
    psT = k.bankT(b)
    pr = "bank%d" % b
    for kc in range(8):
        sc.add("pe", lambda e, kc=kc: e.transpose(psT[:, kc, :], hb[:, kc * 128:(kc + 1) * 128], k.ident),
               reads=[hr, "ident"], writes=[pr])
    sc.add("dve", lambda e: e.tensor_copy(out=hT[:, :, tcol:tcol + 128], in_=psT),
           reads=[pr], writes=[hT_res])


def norm_transpose_tile(k, i, gpre, gres, hT, hT_res, tcol, psT_banks, src=None, src_res=None):
    hb, hr = norm_tile(k, i, gpre, gres, src, src_res)
    transpose_tile(k, i, hb, hr, hT, hT_res, tcol, psT_banks)


def post_norm_residual(k, i, psD, psD_res, gpost, gres):
    sc = k.sc
    s2 = i % 2
    yt = k.yt[s2]
    yr = "yt%d" % s2
    rs, rn = rms_stats(k, psD, psD_res, 2 + s2, yt.bitcast(BF16)[:, 0:D], yr)
    sc.add("dve", lambda e: e.scalar_tensor_tensor(out=yt, in0=psD, scalar=rs, in1=gpost,
                                                   op0=ALU.mult, op1=ALU.mult),
           reads=list(psD_res) + [rn, gres], writes=[yr])
    sc.add("pool", lambda e: e.dma_start(out=k.xs[i * 128:(i + 1) * 128, :], in_=yt, accum_op=ALU.add),
           reads=[yr], writes=[("xs", i)], dsem=yr)


def load_w(k, dst, src2d, res, nsplit=8):
    per = 8 // nsplit
    for j in range(nsplit):
        k.sc.add("pool", lambda e, j=j: e.dma_start(
            out=dst[:, j * per:(j + 1) * per, :],
            in_=src2d[j * per * 128:(j + 1) * per * 128, :].rearrange("(c p) n -> p c n", p=128)),
            writes=[res], dsem=res)


def cross_phase(k, l):
    sc = k.sc
    ar = k.arena
    m0 = ar.mark()
    wq = ar.alloc([8, D], BF16)
    wo = ar.alloc([8, D], BF16)
    wkv = ar.alloc([8, 2 * D], BF16)
    gpre = ar.alloc([D], F32)
    gpost = ar.alloc([D], F32)
    gmem = ar.alloc([D], F32)
    mT = ar.alloc([8, MEM], BF16)
    kxT = ar.alloc([8, MEM], BF16)
    vx = ar.alloc([2, D], BF16)
    hT = ar.alloc([8, 512], BF16)
    qT = ar.alloc([8, 512], BF16)
    oT = ar.alloc([8, 512], BF16)
    pT = [ar.alloc([512], BF16) for _ in range(4)]
    rec = ar.alloc([512], F32)
    load_w(k, wkv, k.wkv_x[l], "wkv")
    load_w(k, wq, k.wq_x[l], "wq")
    load_w(k, wo, k.wo_x[l], "wo")
    load_bcast(k, gmem, k.g_mem[l:l + 1, :], "gmem", "gmem")
    load_bcast(k, gpre, k.g_x_pre[l:l + 1, :], "gpre", "gpre")
    load_bcast(k, gpost, k.g_x_post[l:l + 1, :], "gpost", "gpost")
    for mt in range(2):
        norm_transpose_tile(k, mt, gmem, "gmem", mT, "mT", mt * 128, (0,),
                            src=k.mem[mt * 128:(mt + 1) * 128, :], src_res="mem")
    for m in range(8):
        b = 1 + m % 2
        ps = k.bank(b)[:, 0:MEM]
        for kc in range(8):
            sc.add("pe", lambda e, kc=kc, m=m, ps=ps: e.matmul(
                ps, wkv[:, kc, m * 128:(m + 1) * 128], mT[:, kc, :], start=(kc == 0), stop=(kc == 7)),
                reads=["wkv", "mT"], writes=["bank%d" % b])
        sc.add("dve", lambda e, m=m, ps=ps: e.tensor_copy(out=kxT[:, m, :], in_=ps),
               reads=["bank%d" % b], writes=["kxT"])
    for mt in range(2):
        for half in range(2):
            b = 1 + half
            ps = k.bank(b)
            for kc in range(8):
                sc.add("pe", lambda e, kc=kc, mt=mt, half=half, ps=ps: e.matmul(
                    ps, mT[:, kc, mt * 128:(mt + 1) * 128], wkv[:, kc, D + half * 512:D + (half + 1) * 512],
                    start=(kc == 0), stop=(kc == 7)),
                    reads=["wkv", "mT"], writes=["bank%d" % b])
            sc.add("act", lambda e, mt=mt, half=half, ps=ps: e.activation(
                out=vx[:, mt, half * 512:(half + 1) * 512], in_=ps, func=AF.Copy),
                reads=["bank%d" % b], writes=["vx"])

    def normT(c):
        pend = None
        for t in range(4):
            cur = (t,) + norm_tile(k, 4 * c + t, gpre, "gpre")
            if pend is not None:
                transpose_tile(k, 4 * c + pend[0], pend[1], pend[2], hT, "hT", pend[0] * 128, (0,))
            pend = cur
        transpose_tile(k, 4 * c + pend[0], pend[1], pend[2], hT, "hT", pend[0] * 128, (0,))

    def attn(c):
        for m in range(8):
            b = 1 + m % 2
            ps = k.bank(b)
            for kc in range(8):
                sc.add("pe", lambda e, kc=kc, m=m, ps=ps: e.matmul(
                    ps, wq[:, kc, m * 128:(m + 1) * 128], hT[:, kc, :], start=(kc == 0), stop=(kc == 7)),
                    reads=["wq", "hT"], writes=["bank%d" % b])
            sc.add("act", lambda e, m=m, ps=ps: e.activation(out=qT[:, m, :], in_=ps, func=AF.Copy, scale=1.0 / 16.0),
                   reads=["bank%d" % b], writes=["qT"])

        def stA(h):
            for mt in range(2):
                b = 1 + 2 * (h % 2) + mt
                ps = k.bank(b)
                for j in range(2):
                    sc.add("pe", lambda e, j=j, mt=mt, ps=ps: e.matmul(
                        ps, kxT[:, 2 * h + j, mt * 128:(mt + 1) * 128], qT[:, 2 * h + j, :],
                        start=(j == 0), stop=(j == 1)),
                        reads=["kxT", "qT"], writes=["bank%d" % b])

        def stB(h):
            for mt in range(2):
                b = 1 + 2 * (h % 2) + mt
                sl = 2 * (h % 2) + mt
                sc.add("act", lambda e, b=b, sl=sl: e.activation(out=pT[sl], in_=k.bank(b), func=AF.Exp),
                       reads=["bank%d" % b], writes=["pT%d" % sl])

        def stC(h):
            sls = [2 * (h % 2) + mt for mt in range(2)]
            for mt in range(2):
                sc.add("pe", lambda e, mt=mt: e.matmul(k.bank(5), k.ones, pT[sls[mt]], start=(mt == 0), stop=(mt == 1)),
                       reads=["ones", "pT%d" % sls[mt]], writes=["bank5"])
            for dj in range(2):
                for mt in range(2):
                    sc.add("pe", lambda e, mt=mt, dj=dj: e.matmul(
                        k.bank(6 + dj), vx[:, mt, h * 256 + dj * 128:h * 256 + (dj + 1) * 128], pT[sls[mt]],
                        start=(mt == 0), stop=(mt == 1)),
                        reads=["vx", "pT%d" % sls[mt]], writes=["bank%d" % (6 + dj)])
            sc.add("act", lambda e: e.activation(out=rec, in_=k.bank(5), func=AF.Ln), reads=["bank5"], writes=["rec"])
            sc.add("act", lambda e: e.activation(out=rec, in_=rec, func=AF.Exp, scale=-1.0), reads=["rec"], writes=["rec"])
            for dj in range(2):
                sc.add("dve", lambda e, dj=dj: e.tensor_tensor(
                    out=oT[:, 2 * h + dj, :], in0=k.bank(6 + dj), in1=rec, op=ALU.mult),
                    reads=["bank%d" % (6 + dj), "rec"], writes=["oT"])

        stA(0)
        for h in range(4):
            if h + 1 < 4:
                stA(h + 1)
            stB(h)
            stC(h)

    def oproj(c, bg):
        per = -(-len(bg) // 50)
        for t in range(4):
            i = 4 * c + t
            b0 = 1 + 2 * (t % 2)
            psD = k.bank2(b0)
            for half in range(2):
                for kc in range(8):
                    sc.add("pe", lambda e, kc=kc, half=half, t=t, psD=psD: e.matmul(
                        psD[:, half * 512:(half + 1) * 512], oT[:, kc, t * 128:(t + 1) * 128],
                        wo[:, kc, half * 512:(half + 1) * 512], start=(kc == 0), stop=(kc == 7)),
                        reads=["oT", "wo"], writes=["bank%d" % (b0 + half)])
                    sc.replay(bg, per)
            post_norm_residual(k, i, psD, ["bank%d" % b0, "bank%d" % (b0 + 1)], gpost, "gpost")
        sc.replay(bg)

    normT(0)
    for c in range(NCH):
        attn(c)
        oproj(c, sc.capture(lambda c=c: normT(c + 1)) if c + 1 < NCH else [])
    sc.fence()
    ar.reset(m0)


def mix_phase(k, l):
    sc = k.sc
    ar = k.arena
    m0 = ar.mark()
    hb_save = k.hb
    k.hb = k.hb + [ar.alloc([D], BF16) for _ in range(2)]
    win = ar.alloc([8, IN_COLS], BF16)
    wout = ar.alloc([8, D], BF16)
    poolw = ar.alloc([4, 128], BF16)
    pscale = ar.alloc([4], F32)
    gpre = ar.alloc([D], F32)
    gpost = ar.alloc([D], F32)
    bfb4 = ar.alloc([4, 8], F32)
    kT2 = ar.alloc([4, S], BF16)
    vaug = ar.alloc([NT, 8, 65], BF16)
    k.cf = ar.alloc([NCONST - 128 - 1024], F32)
    k.identf = k.cf[:, 0:128]
    k.negtri = k.cf[:, 128:256]
    k.oneslast = k.cf[:, 256:384]
    k.maskneg = k.cf[:, 384:512]
    k.invc = k.cf[:, 512:528]
    k.onesf = k.cf[:, 528:592]
    k.sel = ar.alloc([8, 128], BF16)
    sc.add("dve", lambda e: e.memset(vaug[:, :, :, 64:65], 1.0), writes=["vaug"])
    sc.add("sp", lambda e: e.dma_start(out=k.cf, in_=k.consts[:, 0:NCONST - 1024 - 128]), writes=["cf"], dsem="cf")
    sc.add("dve", lambda e: e.memset(k.sel, 0.0), writes=["cf"])
    sc.add("pool", lambda e: e.dma_start(
        out=k.sel[0:16], in_=k.consts[0:16, NCONST - 1024:NCONST].rearrange("p (h n) -> p h n", h=8)),
        writes=["cf"], dsem="sel")
    c_all = ar.alloc([NT, 8], F32)
    biasG = ar.alloc([NT, 8], F32)
    crefbc = [ar.alloc([8], F32) for _ in range(2)]
    hT = ar.alloc([8, 512], BF16)
    qT2 = ar.alloc([4, 512], BF16)
    dq = ar.alloc([512], BF16)
    catT = ar.alloc([8, 512], BF16)
    uT = ar.alloc([4, 528], F32)
    pA = ar.alloc([528], F32)
    pB = ar.alloc([528], F32)
    tmpc = ar.alloc([16], F32)
    poolT = ar.alloc([4, 512], BF16)
    pT = [ar.alloc([512], BF16) for _ in range(4)]
    oun = [ar.alloc([512], F32) for _ in range(2)]
    zt = ar.alloc([32], F32)
    e1 = ar.alloc([32], F32)
    sp = ar.alloc([32], F32)
    Wsb = ar.alloc([4, 8], F32)
    R = ar.alloc([5, 8], F32)
    crel = ar.alloc([4, 8], F32)
    hib = ar.alloc([4, 8], BF16)
    X = ar.alloc([4, 16], F32)

    load_w(k, win, k.w_in[l], "win")
    load_w(k, wout, k.w_out[l], "wout", nsplit=4)
    sc.add("pool", lambda e: e.dma_start(out=poolw, in_=k.pool_w[l].rearrange("g c d -> c g d")),
           writes=["poolw"], dsem="poolw")
    sc.add("sp", lambda e: e.dma_start(out=pscale, in_=k.pool_scale_t[l]), writes=["pscale"], dsem="pscale")
    load_bcast(k, gpre, k.g_mix_pre[l:l + 1, :], "gpre", "gpre")
    load_bcast(k, gpost, k.g_mix_post[l:l + 1, :], "gpost", "gpost")
    for t in range(4):
        sc.add("sp", lambda e, t=t: e.dma_start(out=bfb4[:, t, :], in_=k.b_forget[l:l + 1, :].partition_broadcast(128)),
               writes=["bfb4"], dsem="bfb4")
    sc.add("dve", lambda e: e.memset(uT[:, :, 0:16], 0.0), writes=["uT"])
    sc.add("dve", lambda e: e.memset(dq, 0.0), writes=["dq"])
    sc.add("dve", lambda e: e.memset(crefbc[0], 0.0), writes=["cref0"])
    sc.add("dve", lambda e: e.memset(R[:, 0, :], 0.0), writes=["R"])

    normed = {}

    def normA(c):
        for t in range(4):
            normed[4 * c + t] = norm_tile(k, 4 * c + t, gpre, "gpre")

    def normB(c):
        for t in range(4):
            hb, hr = normed.pop(4 * c + t)
            transpose_tile(k, 4 * c + t, hb, hr, hT, "hT", t * 128, (0,))

    rot = [0]

    def pbank():
        rot[0] ^= 1
        return 1 + rot[0]

    def proj_fm(col0, evac):
        b = pbank()
        ps = k.bank(b)
        for kc in range(8):
            sc.add("pe", lambda e, kc=kc, ps=ps: e.matmul(ps, win[:, kc, col0:col0 + 128], hT[:, kc, :],
                                                          start=(kc == 0), stop=(kc == 7)),
                   reads=["win", "hT"], writes=["bank%d" % b])
        evac(ps, "bank%d" % b)

    def cchain_a(G):
        for t in range(4):
            for kc in range(8):
                sc.add("pe", lambda e, kc=kc, t=t: e.matmul(
                    k.bank(7)[:, t * 8:(t + 1) * 8], hT[:, kc, t * 128:(t + 1) * 128], win[:, kc, 1536:1544],
                    start=(kc == 0), stop=(kc == 7)),
                    reads=["win", "hT"], writes=["b7fg"])
        sc.add("dve", lambda e: e.tensor_tensor(out=zt, in0=k.bank(7)[:, 0:32], in1=bfb4.rearrange("p a b -> p (a b)"),
                                                op=ALU.add),
               reads=["b7fg", "bfb4"], writes=["zt"])
        sc.add("act", lambda e: e.activation(out=e1, in_=zt, func=AF.Exp, scale=-1.0), reads=["zt"], writes=["e1"])
        sc.add("act", lambda e: e.activation(out=sp, in_=e1, func=AF.Ln, bias=k.onecol), reads=["e1", "onecol"],
               writes=["sp"])

    def cchain_a2(G):
        sc.add("pe", lambda e: e.matmul(k.bank(7)[:, 32:64], k.negtri, sp, start=True, stop=True),
               reads=["sp", "cf"], writes=["b7w"])
        sc.add("dve", lambda e: e.tensor_copy(out=Wsb.rearrange("p a b -> p (a b)"), in_=k.bank(7)[:, 32:64]),
               reads=["b7w"], writes=["Wsb"])
        sc.add("dve", lambda e: e.tensor_copy(out=R[:, 1, :], in_=Wsb[:, 0, :]), reads=["Wsb"], writes=["R"])
        for t in range(2, 5):
            sc.add("dve", lambda e, t=t: e.tensor_tensor(out=R[:, t, :], in0=R[:, t - 1, :], in1=Wsb[:, t - 1, :],
                                                         op=ALU.add),
                   reads=["Wsb", "R"], writes=["R"])

    def cchain_b(G):
        cur = crefbc[G % 2]
        nxt = crefbc[(G + 1) % 2]
        cr, nr = "cref%d" % (G % 2), "cref%d" % ((G + 1) % 2)
        sc.add("pe", lambda e: e.matmul(k.bank(7)[:, 64:104], k.oneslast, R.rearrange("p a b -> p (a b)"),
                                        start=True, stop=True),
               reads=["R", "cf"], writes=["b7r"])
        psR = k.bank(7)[:, 64:104].rearrange("p (a b) -> p a b", a=5)
        sc.add("dve", lambda e: e.tensor_tensor(out=crel, in0=Wsb, in1=psR[:, 0:4, :], op=ALU.add),
               reads=["Wsb", "b7r"], writes=["crel"])
        sc.add("dve", lambda e: e.tensor_tensor(out=c_all[:, 4 * G:4 * G + 4, :], in0=crel,
                                                in1=cur.unsqueeze(1).to_broadcast([128, 4, 8]), op=ALU.add),
               reads=["crel", cr], writes=["c_all"])
        nk = 4 * G + 4
        sc.add("dve", lambda e: e.scalar_tensor_tensor(out=biasG[:, 0:nk, :], in0=c_all[:, 0:nk, :], scalar=-1.0,
                                                       in1=cur.unsqueeze(1).to_broadcast([128, nk, 8]),
                                                       op0=ALU.mult, op1=ALU.add),
               reads=["c_all", cr], writes=["biasG"])
        sc.add("dve", lambda e: e.tensor_tensor(out=nxt, in0=cur, in1=psR[:, 4, :], op=ALU.add),
               reads=[cr, "b7r"], writes=[nr])
        sc.add("dve", lambda e: e.tensor_copy(out=hib, in_=crel), reads=["crel"], writes=["hib"])
        sc.add("dve", lambda e: e.tensor_copy(out=X[:, :, 0:8], in_=hib), reads=["hib"], writes=["X"])
        sc.add("dve", lambda e: e.tensor_tensor(out=X[:, :, 8:16], in0=crel, in1=X[:, :, 0:8], op=ALU.subtract),
               reads=["crel", "X"], writes=["X"])

    def cchain_b2(G):
        b = pbank()
        for t in range(4):
            sc.add("pe", lambda e, t=t, b=b: e.transpose(k.bank(b)[0:16, t * 128:(t + 1) * 128], X[:, t, :], k.identf),
                   reads=["X", "cf"], writes=["bank%d" % b])
        sc.add("dve", lambda e, b=b: e.tensor_copy(out=dq[0:16, :], in_=k.bank(b)[0:16, :]),
               reads=["bank%d" % b], writes=["dq"])


    def projections(G):
        cchain_a(G)
        for p in range(4):
            proj_fm(p * 128, lambda ps, br, p=p: sc.add(
                "act", lambda e: e.activation(out=qT2[:, p, :], in_=ps, func=AF.Copy, scale=0.125),
                reads=[br], writes=["qT2"]))
        cchain_a2(G)
        for p in range(4):
            proj_fm(512 + p * 128, lambda ps, br, p=p: sc.add(
                "dve", lambda e: e.tensor_copy(out=kT2[:, p, G * 512:(G + 1) * 512], in_=ps),
                reads=[br], writes=["kT2"]))
        cchain_b(G)
        for t in range(4):
            b = pbank()
            ps = k.bank(b)
            for kc in range(8):
                sc.add("pe", lambda e, kc=kc, t=t, ps=ps: e.matmul(
                    ps, hT[:, kc, t * 128:(t + 1) * 128], win[:, kc, 1024:1536], start=(kc == 0), stop=(kc == 7)),
                    reads=["win", "hT"], writes=["bank%d" % b])
            sc.add("act", lambda e, t=t, ps=ps: e.activation(
                out=vaug[:, 4 * G + t, :, 0:64], in_=ps.rearrange("p (h d) -> p h d", h=8), func=AF.Copy),
                reads=["bank%d" % b], writes=["vaug"])
        cchain_b2(G)
        if G > 0:
            sc.add("dve", lambda e: e.tensor_copy(out=uT[:, :, 0:16], in_=uT[:, :, 512:528]),
                   reads=["uT"], writes=["uT"])
        for g in range(4):
            proj_fm(1544 + g * 128, lambda ps, br, g=g: sc.add(
                "act", lambda e: e.activation(out=uT[:, g, 16:528], in_=ps, func=AF.Copy),
                reads=[br], writes=["uT"]))

    def pool_mixer(G):
        for g in range(4):
            src = uT[:, g, :]
            L = g + 1
            bufs = [pA, pB]
            names = ["pA", "pB"]
            prev, prev_r = src, "uT"
            lo = 0
            for lev in range(L):
                sh = 1 << lev
                lo2 = lo + sh
                dst, dst_r = bufs[lev % 2], names[lev % 2]
                sc.add("dve", lambda e, dst=dst, prev=prev, lo2=lo2, sh=sh: e.tensor_tensor(
                    out=dst[:, lo2:528], in0=prev[:, lo2:528], in1=prev[:, lo2 - sh:528 - sh], op=ALU.add),
                    reads=[prev_r], writes=[dst_r])
                prev, prev_r, lo = dst, dst_r, lo2
            w = 1 << L
            sc.add("dve", lambda e, g=g, prev=prev, src=src, w=w: e.scalar_tensor_tensor(
                out=poolT[:, g, :], in0=prev[:, 16:528], scalar=1.0 / w, in1=src[:, 16:528],
                op0=ALU.mult, op1=ALU.subtract),
                reads=[prev_r, "uT"], writes=["poolT"])
            if G == 0:
                sc.add("dve", lambda e, prev=prev, w=w: e.tensor_tensor(
                    out=tmpc[:, 0:w - 1], in0=prev[:, 16:16 + w - 1], in1=k.invc[:, 0:w - 1], op=ALU.mult),
                    reads=[prev_r, "cf"], writes=["tmpc"])
                sc.add("dve", lambda e, g=g, src=src, w=w: e.tensor_tensor(
                    out=poolT[:, g, 0:w - 1], in0=tmpc[:, 0:w - 1], in1=src[:, 16:16 + w - 1], op=ALU.subtract),
                    reads=["tmpc", "uT"], writes=["poolT"])
            b = 0
            ps = k.bank(b)
            sc.add("pe", lambda e, g=g, ps=ps: e.matmul(ps, poolw[:, g, :], poolT[:, g, :], start=True, stop=True),
                   reads=["poolw", "poolT"], writes=["bank%d" % b])
            sc.add("act", lambda e, g=g, ps=ps: e.activation(out=catT[:, 4 + g, :], in_=ps, func=AF.Copy,
                                                            scale=pscale[:, g:g + 1]),
                   reads=["bank%d" % b, "pscale"], writes=["catT"])

    def attention(G, bg):
        nkt = 4 * G + 4
        per_it = -(-len(bg) // max(1, 4 * nkt - 3))
        items = [(p, kt) for p in range(4) for kt in range(nkt)]

        def geom(idx):
            p, kt = items[idx]
            j = kt - 4 * G
            sbs = (1 + 2 * (idx % 2), 2 + 2 * (idx % 2))
            pts = (2 * (idx % 2), 2 * (idx % 2) + 1)
            return p, kt, j, 128 * max(j, 0), sbs, pts

        def stageA(idx):
            p, kt, j, c0, sbs, pts = geom(idx)
            for hb_ in range(2):
                pb = 64 * hb_
                psS = k.bank(sbs[hb_])
                sc.add("pe", lambda e, pb=pb, psS=psS: e.matmul(
                    psS[:, c0:512], kT2[pb:pb + 64, p, kt * 128:(kt + 1) * 128], qT2[pb:pb + 64, p, c0:512],
                    start=True, stop=False, tile_position=(pb, 0)),
                    reads=["kT2", "qT2"], writes=["bank%d" % sbs[hb_]])
            for hb_ in range(2):
                h = 2 * p + hb_
                psS = k.bank(sbs[hb_])
                sc.add("pe", lambda e, h=h, psS=psS: e.matmul(
                    psS[:, c0:512], k.sel[:, h, :], dq[:, c0:512], start=False, stop=True),
                    reads=["cf", "dq"], writes=["bank%d" % sbs[hb_]])
            if j >= 0:
                for hb_ in range(2):
                    psS = k.bank(sbs[hb_])
                    sc.add("dve", lambda e, psS=psS: e.tensor_tensor(
                        out=psS[:, c0:c0 + 128], in0=psS[:, c0:c0 + 128], in1=k.maskneg, op=ALU.add),
                        reads=["bank%d" % sbs[hb_], "cf"], writes=["bank%d" % sbs[hb_]])

        def stageB(idx):
            p, kt, j, c0, sbs, pts = geom(idx)
            for hb_ in range(2):
                h = 2 * p + hb_
                psS = k.bank(sbs[hb_])
                pt = pT[pts[hb_]]
                sc.add("act", lambda e, h=h, psS=psS, pt=pt: e.activation(
                    out=pt[:, c0:512], in_=psS[:, c0:512], func=AF.Exp, bias=biasG[:, kt, h:h + 1]),
                    reads=["bank%d" % sbs[hb_], "biasG"], writes=["pT%d" % pts[hb_]])

        def stageC(idx):
            p, kt, j, c0, sbs, pts = geom(idx)
            for hb_ in range(2):
                h = 2 * p + hb_
                psO = k.bank(5 + hb_)
                pt = pT[pts[hb_]]
                sc.add("pe", lambda e, h=h, psO=psO, pt=pt: e.matmul(
                    psO[0:65, c0:512], vaug[:, kt, h, :], pt[:, c0:512], start=(kt == 0), stop=(kt == nkt - 1)),
                    reads=["vaug", "pT%d" % pts[hb_]], writes=["bank%d" % (5 + hb_)])

        def epi1(p):
            for hb_ in range(2):
                psO = k.bank(5 + hb_)
                sc.add("dve", lambda e, hb_=hb_, psO=psO: e.tensor_copy(out=oun[hb_][0:65, :], in_=psO[0:65, :]),
                       reads=["bank%d" % (5 + hb_)], writes=["oun%d" % hb_])
                sc.add("dve", lambda e, hb_=hb_: e.reciprocal(out=oun[hb_][64:65, :], in_=oun[hb_][64:65, :]),
                       reads=["oun%d" % hb_], writes=["oun%d" % hb_])

        def epi2(p):
            for hb_ in range(2):
                pb = 64 * hb_
                sc.add("pe", lambda e, hb_=hb_: e.matmul(k.bank(7)[0:64, :], k.onesf[64:65, 0:64],
                                                          oun[hb_][64:65, :], start=True, stop=True),
                       reads=["oun%d" % hb_, "cf"], writes=["b7fg", "b7w", "b7r"])
                sc.add("dve", lambda e, hb_=hb_, pb=pb: e.tensor_tensor(
                    out=catT[pb:pb + 64, p, :], in0=oun[hb_][0:64, :], in1=k.bank(7)[0:64, :], op=ALU.mult),
                    reads=["oun%d" % hb_, "b7fg", "b7w", "b7r"], writes=["catT"])

        pending = []
        stageA(0)
        for idx in range(len(items)):
            if idx + 1 < len(items):
                stageA(idx + 1)
            stageB(idx)
            stageC(idx)
            sc.replay(bg, per_it)
            for ep in pending:
                ep[1] -= 1
            while pending and pending[0][1] <= 0:
                epi2(pending.pop(0)[0])
            p, kt = items[idx]
            if kt == nkt - 1:
                epi1(p)
                pending.append([p, 3])
        while pending:
            epi2(pending.pop(0)[0])
        sc.replay(bg)

    def outproj(G):
        for t in range(4):
            i = 4 * G + t
            b0 = 1 + 2 * (t % 2)
            psD = k.bank2(b0)
            for half in range(2):
                for kc in range(8):
                    sc.add("pe", lambda e, kc=kc, half=half, t=t, psD=psD: e.matmul(
                        psD[:, half * 512:(half + 1) * 512], catT[:, kc, t * 128:(t + 1) * 128],
                        wout[:, kc, half * 512:(half + 1) * 512], start=(kc == 0), stop=(kc == 7)),
                        reads=["catT", "wout"], writes=["bank%d" % (b0 + half)])
            post_norm_residual(k, i, psD, ["bank%d" % b0, "bank%d" % (b0 + 1)], gpost, "gpost")

    normA(0)
    normB(0)
    for G in range(NCH):
        projections(G)

        def background(G=G):
            pool_mixer(G)
            if G + 1 < NCH:
                normA(G + 1)
                normB(G + 1)
        attention(G, sc.capture(background))
        outproj(G)
    sc.fence()
    k.hb = hb_save
    ar.reset(m0)


def ffn_phase(k, l):
    sc = k.sc
    ar = k.arena
    m0 = ar.mark()
    wup = ar.alloc([8, DFF], BF16)
    wdn = ar.alloc([32, D], BF16)
    gpre = ar.alloc([D], F32)
    gpost = ar.alloc([D], F32)
    hT = ar.alloc([8, 512], BF16)
    aT = ar.alloc([32, 512], BF16)
    r = [ar.alloc([512], BF16) for _ in range(2)]
    for kc in range(8):
        sc.add("pool", lambda e, kc=kc: e.dma_start(out=wup[:, kc, :], in_=k.w_up[l, kc * 128:(kc + 1) * 128, :]),
               writes=["wup"], dsem="wup")
    for j in range(8):
        sc.add("pool", lambda e, j=j: e.dma_start(
            out=wdn[:, 4 * j:4 * j + 4, :],
            in_=k.w_down[l, 512 * j:512 * (j + 1), :].rearrange("(f p) n -> p f n", p=128)),
            writes=["wdn"], dsem="wdn")
    load_bcast(k, gpre, k.g_ffn_pre[l:l + 1, :], "gpre", "gpre")
    load_bcast(k, gpost, k.g_ffn_post[l:l + 1, :], "gpost", "gpost")

    def normT(c):
        pend = None
        for t in range(4):
            cur = (t,) + norm_tile(k, 4 * c + t, gpre, "gpre")
            if pend is not None:
                transpose_tile(k, 4 * c + pend[0], pend[1], pend[2], hT, "hT", pend[0] * 128, (0, 1))
            pend = cur
        transpose_tile(k, 4 * c + pend[0], pend[1], pend[2], hT, "hT", pend[0] * 128, (0, 1))

    def up(c):
        for f in range(32):
            ps = k.bank(2 + f % 2)
            pr = "bank%d" % (2 + f % 2)
            for kc in range(8):
                sc.add("pe", lambda e, kc=kc, f=f, ps=ps: e.matmul(
                    ps, wup[:, kc, f * 128:(f + 1) * 128], hT[:, kc, :], start=(kc == 0), stop=(kc == 7)),
                    reads=["wup", "hT"], writes=[pr])
            rr = r[f % 2]
            rres = "r%d" % (f % 2)
            sc.add("act", lambda e, ps=ps, rr=rr: e.activation(out=rr, in_=ps, func=AF.Relu),
                   reads=[pr], writes=[rres])
            sc.add("dve", lambda e, f=f, rr=rr: e.tensor_tensor(out=aT[:, f, :], in0=rr, in1=rr, op=ALU.mult),
                   reads=[rres], writes=["aT"])

    def down(c, bg):
        per = -(-len(bg) // 200)
        for t in range(4):
            i = 4 * c + t
            b0 = 4 + 2 * (i % 2)
            psD = k.bank2(b0)
            for half in range(2):
                for f in range(32):
                    sc.add("pe", lambda e, f=f, half=half, psD=psD, t=t: e.matmul(
                        psD[:, half * 512:(half + 1) * 512], aT[:, f, t * 128:(t + 1) * 128],
                        wdn[:, f, half * 512:(half + 1) * 512], start=(f == 0), stop=(f == 31)),
                        reads=["aT", "wdn"], writes=["bank%d" % (b0 + half)])
                    sc.replay(bg, per)
            post_norm_residual(k, i, psD, ["bank%d" % b0, "bank%d" % (b0 + 1)], gpost, "gpost")
        sc.replay(bg)

    normT(0)
    for c in range(NCH):
        up(c)
        down(c, sc.capture(lambda c=c: normT(c + 1)) if c + 1 < NCH else [])
    sc.fence()
    ar.reset(m0)


def build_program(phases=("mix", "cross", "ffn"), layers=range(DEPTH)):
    nc = bass.Bass("TRN2", target_bir_lowering=False)
    k = K()
    k.nc = nc

    def din(name, shape):
        return nc.dram_tensor(name, list(shape), F32, kind="ExternalInput").ap()

    k.x = din("x", [S, D])
    k.mem = din("mem", [MEM, D])
    k.g_mix_pre = din("g_mix_pre", [DEPTH, D])
    k.w_in = din("w_in", [DEPTH, D, IN_COLS])
    k.b_forget = din("b_forget", [DEPTH, 8])
    k.pool_w = din("pool_w", [DEPTH, 4, 128, 128])
    k.pool_scale = din("pool_scale", [DEPTH, 512])
    k.w_out = din("w_out", [DEPTH, D, D])
    k.g_mix_post = din("g_mix_post", [DEPTH, D])
    k.g_x_pre = din("g_x_pre", [DEPTH, D])
    k.g_mem = din("g_mem", [DEPTH, D])
    k.wq_x = din("wq_x", [DEPTH, D, D])
    k.wkv_x = din("wkv_x", [DEPTH, D, 2 * D])
    k.wo_x = din("wo_x", [DEPTH, D, D])
    k.g_x_post = din("g_x_post", [DEPTH, D])
    k.g_ffn_pre = din("g_ffn_pre", [DEPTH, D])
    k.w_up = din("w_up", [DEPTH, D, DFF])
    k.w_down = din("w_down", [DEPTH, DFF, D])
    k.g_ffn_post = din("g_ffn_post", [DEPTH, D])
    k.consts = din("consts", [128, NCONST])
    k.pool_scale_t = din("pool_scale_t", [DEPTH, 128, 4])
    k.xs = nc.dram_tensor("out", [S, D], F32, kind="ExternalOutput").ap()

    sc = Sched(nc)
    k.sc = sc
    ar = Arena(nc, 206 * 1024)
    k.arena = ar
    k.ident = ar.alloc([128], BF16)
    k.ones = ar.alloc([128], BF16)
    k.onecol = ar.alloc([1], F32)
    k.epsb = ar.alloc([1], F32)
    k.stat = ar.alloc([4, 4], F32)
    k.xin = [ar.alloc([D], F32) for _ in range(2)]
    k.hb = [ar.alloc([D], BF16) for _ in range(2)]
    k.yt = [ar.alloc([D], F32) for _ in range(2)]
    k.ps = nc.alloc_psum_tensor("ps", [128, 8 * 512], F32)
    k.bank = lambda b: k.ps[:, b * 512:(b + 1) * 512]
    k.bank2 = lambda b: k.ps[:, b * 512:(b + 2) * 512]
    k.bankT = lambda b: k.ps[:, b * 512:(b + 1) * 512].bitcast(BF16).rearrange("p (a b) -> p a b", a=8)

    sc.add("pool", lambda e: e.dma_start(out=k.ident, in_=k.consts[:, NCONST - 1024 - 128:NCONST - 1024]),
           writes=["ident"], dsem="ident")
    sc.add("dve", lambda e: e.memset(k.epsb, EPS), writes=["epsb"])
    sc.add("dve", lambda e: e.memset(k.ones, 1.0), writes=["ones"])
    sc.add("dve", lambda e: e.memset(k.onecol, 1.0), writes=["onecol"])

    for c in range(NCH):
        sc.add("sp", lambda e, c=c: e.dma_start(out=k.xs[c * 512:(c + 1) * 512, :], in_=k.x[c * 512:(c + 1) * 512, :]),
               writes=[("xs", 4 * c + t) for t in range(4)], dsem="xcopy%d" % c)
    for l in layers:
        if "mix" in phases:
            mix_phase(k, l)
        if "cross" in phases:
            cross_phase(k, l)
        if "ffn" in phases:
            ffn_phase(k, l)
    final = sorted(set(o.dsem for o in sc.ops if o.isdma), key=str)
    sc.emit(final_dsems=final)
    return nc


INPUT_NAMES = ["x", "mem", "g_mix_pre", "w_in", "b_forget", "pool_w", "pool_scale", "w_out", "g_mix_post",
               "g_x_pre", "g_mem", "wq_x", "wkv_x", "wo_x", "g_x_post", "g_ffn_pre", "w_up", "w_down",
               "g_ffn_post"]


def make_consts():
    c = np.zeros((128, NCONST), np.float32)
    eye = np.eye(128, dtype=np.float32)
    idx = np.arange(128)
    c[:, 0:128] = eye
    c[:, 128:256] = -(idx[:, None] <= idx[None, :]).astype(np.float32)
    c[127, 256:384] = 1.0
    c[:, 384:512] = np.where(idx[:, None] <= idx[None, :], 0.0, -30000.0)
    c[:, 512:528] = 1.0 / (np.arange(16) + 1.0)[None, :]
    c[:, 528:592] = 1.0
    c[:, 592:720] = eye
    selbase = 720
    for h in range(8):
        c[h, selbase + h * 128:selbase + (h + 1) * 128] = 1.0
        c[8 + h, selbase + h * 128:selbase + (h + 1) * 128] = 1.0
    return c


def kernel(**inputs):
    nc = build_program()
    consts = make_consts()
    in_maps = []
    for b in range(N_CORES):
        m = {}
        for n in INPUT_NAMES:
            a = np.asarray(inputs[n], dtype=np.float32)
            if n in ("x", "mem"):
                a = a[b]
            m[n] = np.ascontiguousarray(a)
        m["consts"] = consts
        m["pool_scale_t"] = np.ascontiguousarray(
            np.asarray(inputs["pool_scale"], dtype=np.float32).reshape(DEPTH, 4, 128).transpose(0, 2, 1))
        in_maps.append(m)
    res = run_bass_kernel_spmd(nc, in_maps, core_ids=list(range(N_CORES)))
    return np.stack([np.asarray(r["out"], dtype=np.float32) for r in res.results], axis=0)
```

```python
import numpy as np
import concourse.bass as bass
import concourse.mybir as mybir
from concourse.bass_utils import run_bass_kernel_spmd

F32 = mybir.dt.float32
BF16 = mybir.dt.bfloat16
U8 = mybir.dt.uint8
AF = mybir.ActivationFunctionType
ALU = mybir.AluOpType

D = 1024
S = 4096
DEPTH = 4
NT = S // 128
NCH = S // 512
DFF = 4096
MEM = 256
EPS = 1e-6
IN_COLS = 2056
N_CORES = 8
NCONST = 592 + 128 + 1024


class Op:
    __slots__ = ("eng", "fn", "deps", "marked", "count", "dsem", "idx", "isdma")


class Sched:
    ENGS = ("pe", "act", "dve", "pool", "sp")

    def __init__(self, nc):
        self.nc = nc
        self.ops = []
        self.last_w = {}
        self.readers = {}
        self.dsem_counts = {}
        self.fence_deps = {e: [] for e in self.ENGS}
        self.capturing = None

    def capture(self, fn):
        assert self.capturing is None
        self.capturing = []
        fn()
        lst, self.capturing = self.capturing, None
        return lst

    def replay(self, lst, n=None):
        n = len(lst) if n is None else min(n, len(lst))
        for _ in range(n):
            self.add(*lst.pop(0))

    def add(self, eng, fn, reads=(), writes=(), dsem=None):
        if self.capturing is not None:
            self.capturing.append((eng, fn, tuple(reads), tuple(writes), dsem))
            return None
        op = Op()
        op.eng = eng
        op.fn = fn
        op.dsem = dsem
        op.isdma = dsem is not None
        op.idx = len(self.ops)
        op.marked = False
        op.count = 0
        deps = list(self.fence_deps[eng])
        self.fence_deps[eng] = []
        for r in reads:
            w = self.last_w.get(r)
            if w is not None:
                deps.append(w)
        for r in writes:
            w = self.last_w.get(r)
            if w is not None:
                deps.append(w)
            deps.extend(self.readers.get(r, ()))
        for r in reads:
            self.readers.setdefault(r, []).append(op)
        for r in writes:
            self.last_w[r] = op
            self.readers[r] = []
        best = {}
        for d in deps:
            if d is op:
                continue
            key = ("d", d.dsem) if d.isdma else ("e", d.eng)
            if key not in best or best[key].idx < d.idx:
                best[key] = d
        if eng == "pe" and not op.isdma:
            best.pop(("e", "pe"), None)
        op.deps = list(best.values())
        for d in op.deps:
            d.marked = True
        if op.isdma:
            self.dsem_counts[dsem] = self.dsem_counts.get(dsem, 0) + 16
            op.count = self.dsem_counts[dsem]
        self.ops.append(op)
        return op

    def fence(self):
        last = {}
        for op in self.ops:
            key = ("d", op.dsem) if op.isdma else ("e", op.eng)
            last[key] = op
        for e in self.ENGS:
            self.fence_deps[e] = list(last.values())

    def emit(self, final_dsems=()):
        nc = self.nc
        cnt = {e: 0 for e in self.ENGS}
        for op in self.ops:
            if not op.isdma and op.marked:
                cnt[op.eng] += 1
                op.count = cnt[op.eng]
        esem = {e: nc.alloc_semaphore("s_" + e) for e in self.ENGS if e != "sp"}
        dsem = {k: nc.alloc_semaphore("d_%d" % i) for i, k in enumerate(self.dsem_counts)}
        per_eng = {e: [o for o in self.ops if o.eng == e] for e in self.ENGS}

        def run(e, handle):
            waited = {}
            for op in per_eng[e]:
                for d in op.deps:
                    sem = dsem[d.dsem] if d.isdma else esem[d.eng]
                    if waited.get(sem, 0) < d.count:
                        handle.wait_ge(sem, d.count)
                        waited[sem] = d.count
                ins = op.fn(handle)
                if op.isdma:
                    ins.then_inc(dsem[op.dsem], 16)
                elif op.marked:
                    ins.then_inc(esem[e], 1)
            if e == "sp":
                for k in final_dsems:
                    handle.wait_ge(dsem[k], self.dsem_counts[k])

        with nc.Block() as block:
            @block.sync
            def _(h):
                run("sp", h)

            @block.tensor
            def _(h):
                run("pe", h)

            @block.vector
            def _(h):
                run("dve", h)

            @block.scalar
            def _(h):
                run("act", h)

            @block.gpsimd
            def _(h):
                run("pool", h)


class Arena:
    def __init__(self, nc, nbytes):
        self.t = nc.alloc_sbuf_tensor("arena", [128, nbytes], U8)
        self.nbytes = nbytes
        self.off = 0

    def mark(self):
        return self.off

    def alloc_at(self, off, shape, dtype):
        save = self.off
        self.off = off
        ap = self.alloc(shape, dtype)
        end = self.off
        self.off = save
        return ap, end

    def reset(self, off):
        self.off = off

    def alloc(self, shape, dtype):
        esz = 4 if dtype == F32 else 2
        n = 1
        for s in shape:
            n *= s
        nb = (n * esz + 63) // 64 * 64
        assert self.off + nb <= self.nbytes, ("SBUF arena overflow", self.off + nb, self.nbytes)
        ap = self.t[:, self.off:self.off + n * esz].bitcast(dtype)
        self.off += nb
        if len(shape) == 2:
            ap = ap.rearrange("p (a b) -> p a b", a=shape[0])
        elif len(shape) == 3:
            ap = ap.rearrange("p (a b c) -> p a b c", a=shape[0], b=shape[1])
        return ap


class K:
    pass


def rms_stats(k, src_ap, src_res, slot, junk, junk_res):
    sc = k.sc
    ss = k.stat[:, slot, 0:1]
    ln = k.stat[:, slot, 1:2]
    rs = k.stat[:, slot, 2:3]
    rn = "stat%d" % slot
    sc.add("act", lambda e: e.activation(out=junk, in_=src_ap, func=AF.Square, accum_out=ss),
           reads=(list(src_res) if isinstance(src_res, list) else [src_res]), writes=[rn, junk_res])
    sc.add("act", lambda e: e.activation(out=ln, in_=ss, func=AF.Ln, scale=1.0 / D, bias=k.epsb),
           reads=[rn, "epsb"], writes=[rn])
    sc.add("act", lambda e: e.activation(out=rs, in_=ln, func=AF.Exp, scale=-0.5),
           reads=[rn], writes=[rn])
    return rs, rn


def load_bcast(k, dst, src_row, res, dsem):
    k.sc.add("sp", lambda e: e.dma_start(out=dst, in_=src_row.partition_broadcast(128)),
             writes=[res], dsem=dsem)


def norm_tile(k, i, gpre, gres, src=None, src_res=None):
    sc = k.sc
    s2 = i % 2
    xin = k.xin[s2]
    xr = "xin%d" % s2
    if src is None:
        src = k.xs[i * 128:(i + 1) * 128, :]
        src_res = ("xs", i)
    sc.add("sp", lambda e: e.dma_start(out=xin, in_=src),
           reads=[src_res], writes=[xr], dsem=xr)
    sh = i % len(k.hb)
    hb = k.hb[sh]
    hr = "hb%d" % sh
    rs, rn = rms_stats(k, xin, xr, s2, hb, hr)
    sc.add("dve", lambda e: e.scalar_tensor_tensor(out=hb, in0=xin, scalar=rs, in1=gpre,
                                                   op0=ALU.mult, op1=ALU.mult),
           reads=[xr, rn, gres], writes=[hr])
    return hb, hr


def transpose_tile(k, i, hb, hr, hT, hT_res, tcol, psT_banks):
    sc = k.sc
    b = psT_banks[i % len(psT_banks)]
    psT = k.bankT(b)
    pr = "bank%d" % b
    for kc in range(8):
        sc.add("pe", lambda e, kc=kc: e.transpose(psT[:, kc, :], hb[:, kc * 128:(kc + 1) * 128], k.ident),
               reads=[hr, "ident"], writes=[pr])
    sc.add("dve", lambda e: e.tensor_copy(out=hT[:, :, tcol:tcol + 128], in_=psT),
           reads=[pr], writes=[hT_res])


def norm_transpose_tile(k, i, gpre, gres, hT, hT_res, tcol, psT_banks, src=None, src_res=None):
    hb, hr = norm_tile(k, i, gpre, gres, src, src_res)
    transpose_tile(k, i, hb, hr, hT, hT_res, tcol, psT_banks)


def post_norm_residual(k, i, psD, psD_res, gpost, gres):
    sc = k.sc
    s2 = i % 2
    yt = k.yt[s2]
    yr = "yt%d" % s2
    rs, rn = rms_stats(k, psD, psD_res, 2 + s2, yt.bitcast(BF16)[:, 0:D], yr)
    sc.add("dve", lambda e: e.scalar_tensor_tensor(out=yt, in0=psD, scalar=rs, in1=gpost,
                                                   op0=ALU.mult, op1=ALU.mult),
           reads=list(psD_res) + [rn, gres], writes=[yr])
    sc.add("pool", lambda e: e.dma_start(out=k.xs[i * 128:(i + 1) * 128, :], in_=yt, accum_op=ALU.add),
           reads=[yr], writes=[("xs", i)], dsem=yr)


def load_w(k, dst, src2d, res, nsplit=8):
    per = 8 // nsplit
    for j in range(nsplit):
        k.sc.add("pool", lambda e, j=j: e.dma_start(
            out=dst[:, j * per:(j + 1) * per, :],
            in_=src2d[j * per * 128:(j + 1) * per * 128, :].rearrange("(c p) n -> p c n", p=128)),
            writes=[res], dsem=res)


def load_w_cols(k, dst, src2d, res, blocks, extra_writes=()):
    for j, (c0, c1) in enumerate(blocks):
        k.sc.add("pool", lambda e, c0=c0, c1=c1: e.dma_start(
            out=dst[:, :, c0:c1], in_=src2d[:, c0:c1].rearrange("(c p) n -> p c n", p=128)),
            writes=["%s%d" % (res, j)] + list(extra_writes), dsem="%s%d" % (res, j))


WUP_OFF = 206 * 1024 - 8 * DFF * 2


def cross_phase(k, l):
    sc = k.sc
    ar = k.arena
    m0 = ar.mark()
    wq = ar.alloc([8, D], BF16)
    wo = ar.alloc([8, D], BF16)
    wkv, _ = ar.alloc_at(WUP_OFF, [8, 2 * D], BF16)
    gpre = ar.alloc([D], F32)
    gpost = ar.alloc([D], F32)
    gmem = ar.alloc([D], F32)
    mT = ar.alloc([8, MEM], BF16)
    kxT = ar.alloc([8, MEM], BF16)
    vx = ar.alloc([2, D], BF16)
    hT = ar.alloc([8, 512], BF16)
    qT = ar.alloc([8, 512], BF16)
    oT = ar.alloc([8, 512], BF16)
    pT = [ar.alloc([512], BF16) for _ in range(4)]
    rec = ar.alloc([512], F32)
    load_w_cols(k, wkv, k.wkv_x[l], "wkv", [(j * 512, (j + 1) * 512) for j in range(4)])
    load_w_cols(k, wq, k.wq_x[l], "wq", [(0, 512), (512, 1024)])
    load_w_cols(k, wo, k.wo_x[l], "wo", [(0, 512), (512, 1024)])
    load_bcast(k, gmem, k.g_mem[l:l + 1, :], "gmem", "gmem")
    load_bcast(k, gpre, k.g_x_pre[l:l + 1, :], "gpre", "gpre")
    load_bcast(k, gpost, k.g_x_post[l:l + 1, :], "gpost", "gpost")
    for mt in range(2):
        norm_transpose_tile(k, mt, gmem, "gmem", mT, "mT", mt * 128, (0,),
                            src=k.mem[mt * 128:(mt + 1) * 128, :], src_res="mem")
    for m in range(8):
        b = 1 + m % 2
        ps = k.bank(b)[:, 0:MEM]
        for kc in range(8):
            sc.add("pe", lambda e, kc=kc, m=m, ps=ps: e.matmul(
                ps, wkv[:, kc, m * 128:(m + 1) * 128], mT[:, kc, :], start=(kc == 0), stop=(kc == 7)),
                reads=["wkv%d" % (m // 4), "mT"], writes=["bank%d" % b])
        sc.add("dve", lambda e, m=m, ps=ps: e.tensor_copy(out=kxT[:, m, :], in_=ps),
               reads=["bank%d" % b], writes=["kxT"])
    for mt in range(2):
        for half in range(2):
            b = 1 + half
            ps = k.bank(b)
            for kc in range(8):
                sc.add("pe", lambda e, kc=kc, mt=mt, half=half, ps=ps: e.matmul(
                    ps, mT[:, kc, mt * 128:(mt + 1) * 128], wkv[:, kc, D + half * 512:D + (half + 1) * 512],
                    start=(kc == 0), stop=(kc == 7)),
                    reads=["wkv%d" % (2 + half), "mT"], writes=["bank%d" % b])
            sc.add("act", lambda e, mt=mt, half=half, ps=ps: e.activation(
                out=vx[:, mt, half * 512:(half + 1) * 512], in_=ps, func=AF.Copy),
                reads=["bank%d" % b], writes=["vx"])

    def normT(c):
        pend = None
        for t in range(4):
            cur = (t,) + norm_tile(k, 4 * c + t, gpre, "gpre")
            if pend is not None:
                transpose_tile(k, 4 * c + pend[0], pend[1], pend[2], hT, "hT", pend[0] * 128, (0,))
            pend = cur
        transpose_tile(k, 4 * c + pend[0], pend[1], pend[2], hT, "hT", pend[0] * 128, (0,))

    def attn(c):
        for m in range(8):
            b = 1 + m % 2
            ps = k.bank(b)
            for kc in range(8):
                sc.add("pe", lambda e, kc=kc, m=m, ps=ps: e.matmul(
                    ps, wq[:, kc, m * 128:(m + 1) * 128], hT[:, kc, :], start=(kc == 0), stop=(kc == 7)),
                    reads=["wq%d" % (m // 4), "hT"], writes=["bank%d" % b])
            sc.add("act", lambda e, m=m, ps=ps: e.activation(out=qT[:, m, :], in_=ps, func=AF.Copy, scale=1.0 / 16.0),
                   reads=["bank%d" % b], writes=["qT"])

        def stA(h):
            for mt in range(2):
                b = 1 + 2 * (h % 2) + mt
                ps = k.bank(b)
                for j in range(2):
                    sc.add("pe", lambda e, j=j, mt=mt, ps=ps: e.matmul(
                        ps, kxT[:, 2 * h + j, mt * 128:(mt + 1) * 128], qT[:, 2 * h + j, :],
                        start=(j == 0), stop=(j == 1)),
                        reads=["kxT", "qT"], writes=["bank%d" % b])

        def stB(h):
            for mt in range(2):
                b = 1 + 2 * (h % 2) + mt
                sl = 2 * (h % 2) + mt
                sc.add("act", lambda e, b=b, sl=sl: e.activation(out=pT[sl], in_=k.bank(b), func=AF.Exp),
                       reads=["bank%d" % b], writes=["pT%d" % sl])

        def stC(h):
            sls = [2 * (h % 2) + mt for mt in range(2)]
            for mt in range(2):
                sc.add("pe", lambda e, mt=mt: e.matmul(k.bank(5), k.ones, pT[sls[mt]], start=(mt == 0), stop=(mt == 1)),
                       reads=["ones", "pT%d" % sls[mt]], writes=["bank5"])
            for dj in range(2):
                for mt in range(2):
                    sc.add("pe", lambda e, mt=mt, dj=dj: e.matmul(
                        k.bank(6 + dj), vx[:, mt, h * 256 + dj * 128:h * 256 + (dj + 1) * 128], pT[sls[mt]],
                        start=(mt == 0), stop=(mt == 1)),
                        reads=["vx", "pT%d" % sls[mt]], writes=["bank%d" % (6 + dj)])
            sc.add("act", lambda e: e.activation(out=rec, in_=k.bank(5), func=AF.Ln), reads=["bank5"], writes=["rec"])
            sc.add("act", lambda e: e.activation(out=rec, in_=rec, func=AF.Exp, scale=-1.0), reads=["rec"], writes=["rec"])
            for dj in range(2):
                sc.add("dve", lambda e, dj=dj: e.tensor_tensor(
                    out=oT[:, 2 * h + dj, :], in0=k.bank(6 + dj), in1=rec, op=ALU.mult),
                    reads=["bank%d" % (6 + dj), "rec"], writes=["oT"])

        stA(0)
        for h in range(4):
            if h + 1 < 4:
                stA(h + 1)
            stB(h)
            stC(h)

    def oproj(c, bg):
        per = -(-len(bg) // 50)
        for t in range(4):
            i = 4 * c + t
            b0 = 1 + 2 * (t % 2)
            psD = k.bank2(b0)
            for half in range(2):
                for kc in range(8):
                    sc.add("pe", lambda e, kc=kc, half=half, t=t, psD=psD: e.matmul(
                        psD[:, half * 512:(half + 1) * 512], oT[:, kc, t * 128:(t + 1) * 128],
                        wo[:, kc, half * 512:(half + 1) * 512], start=(kc == 0), stop=(kc == 7)),
                        reads=["oT", "wo%d" % half], writes=["bank%d" % (b0 + half)])
                    sc.replay(bg, per)
            post_norm_residual(k, i, psD, ["bank%d" % b0, "bank%d" % (b0 + 1)], gpost, "gpost")
        sc.replay(bg)

    if k.prefetch_wup:
        wup, _ = ar.alloc_at(WUP_OFF, [8, DFF], BF16)
        assert ar.off <= WUP_OFF, ("cross phase collides with w_up home", ar.off, WUP_OFF)
        load_w_cols(k, wup, k.w_up[l], "wup", [(j * 512, (j + 1) * 512) for j in range(8)],
                    extra_writes=["wkv%d" % j for j in range(4)])
        k.wup_loaded = l
    normT(0)
    for c in range(NCH):
        attn(c)
        oproj(c, sc.capture(lambda c=c: normT(c + 1)) if c + 1 < NCH else [])
    sc.fence()
    ar.reset(m0)


def mix_phase(k, l):
    sc = k.sc
    ar = k.arena
    m0 = ar.mark()
    hb_save = k.hb
    k.hb = k.hb + [ar.alloc([D], BF16) for _ in range(2)]
    win = ar.alloc([8, IN_COLS], BF16)
    wout = ar.alloc([8, D], BF16)
    poolw = ar.alloc([4, 128], BF16)
    pscale = ar.alloc([4], F32)
    gpre = ar.alloc([D], F32)
    gpost = ar.alloc([D], F32)
    bfb4 = ar.alloc([4, 8], F32)
    kT2 = ar.alloc([4, S], BF16)
    vaug = ar.alloc([NT, 8, 65], BF16)
    k.cf = ar.alloc([NCONST - 128 - 1024], F32)
    k.identf = k.cf[:, 0:128]
    k.negtri = k.cf[:, 128:256]
    k.oneslast = k.cf[:, 256:384]
    k.maskneg = k.cf[:, 384:512]
    k.invc = k.cf[:, 512:528]
    k.onesf = k.cf[:, 528:592]
    k.sel = ar.alloc([8, 128], BF16)
    sc.add("dve", lambda e: e.memset(vaug[:, :, :, 64:65], 1.0), writes=["vaug"])
    sc.add("sp", lambda e: e.dma_start(out=k.cf, in_=k.consts[:, 0:NCONST - 1024 - 128]), writes=["cf"], dsem="cf")
    sc.add("dve", lambda e: e.memset(k.sel, 0.0), writes=["cf"])
    sc.add("pool", lambda e: e.dma_start(
        out=k.sel[0:16], in_=k.consts[0:16, NCONST - 1024:NCONST].rearrange("p (h n) -> p h n", h=8)),
        writes=["cf"], dsem="sel")
    c_all = ar.alloc([NT, 8], F32)
    biasG = ar.alloc([NT, 8], F32)
    crefbc = [ar.alloc([8], F32) for _ in range(2)]
    hT = ar.alloc([8, 512], BF16)
    qT2 = ar.alloc([4, 512], BF16)
    dq = ar.alloc([512], BF16)
    catT = ar.alloc([8, 512], BF16)
    uT = ar.alloc([4, 528], F32)
    pA = ar.alloc([528], F32)
    pB = ar.alloc([528], F32)
    tmpc = ar.alloc([16], F32)
    poolT = ar.alloc([4, 512], BF16)
    pT = [ar.alloc([512], BF16) for _ in range(4)]
    oun = [ar.alloc([512], F32) for _ in range(2)]
    zt = ar.alloc([32], F32)
    e1 = ar.alloc([32], F32)
    sp = ar.alloc([32], F32)
    Wsb = ar.alloc([4, 8], F32)
    R = ar.alloc([5, 8], F32)
    crel = ar.alloc([4, 8], F32)
    hib = ar.alloc([4, 8], BF16)
    X = ar.alloc([4, 16], F32)

    load_w_cols(k, win, k.w_in[l], "win", [(1024, 1544), (0, 512), (512, 1024), (1544, 2056)])
    load_w_cols(k, wout, k.w_out[l], "wout", [(0, 512), (512, 1024)])
    sc.add("pool", lambda e: e.dma_start(out=poolw, in_=k.pool_w[l].rearrange("g c d -> c g d")),
           writes=["poolw"], dsem="poolw")
    sc.add("sp", lambda e: e.dma_start(out=pscale, in_=k.pool_scale_t[l]), writes=["pscale"], dsem="pscale")
    load_bcast(k, gpre, k.g_mix_pre[l:l + 1, :], "gpre", "gpre")
    load_bcast(k, gpost, k.g_mix_post[l:l + 1, :], "gpost", "gpost")
    for t in range(4):
        sc.add("sp", lambda e, t=t: e.dma_start(out=bfb4[:, t, :], in_=k.b_forget[l:l + 1, :].partition_broadcast(128)),
               writes=["bfb4"], dsem="bfb4")
    sc.add("dve", lambda e: e.memset(uT[:, :, 0:16], 0.0), writes=["uT"])
    sc.add("dve", lambda e: e.memset(dq, 0.0), writes=["dq"])
    sc.add("dve", lambda e: e.memset(crefbc[0], 0.0), writes=["cref0"])
    sc.add("dve", lambda e: e.memset(R[:, 0, :], 0.0), writes=["R"])

    normed = {}

    def normA(c):
        for t in range(4):
            normed[4 * c + t] = norm_tile(k, 4 * c + t, gpre, "gpre")

    def normB(c):
        for t in range(4):
            hb, hr = normed.pop(4 * c + t)
            transpose_tile(k, 4 * c + t, hb, hr, hT, "hT", t * 128, (0,))

    rot = [0]

    def pbank():
        rot[0] ^= 1
        return 1 + rot[0]

    def proj_fm(col0, evac):
        b = pbank()
        ps = k.bank(b)
        wres = "win1" if col0 < 512 else ("win2" if col0 < 1024 else "win3")
        for kc in range(8):
            sc.add("pe", lambda e, kc=kc, ps=ps: e.matmul(ps, win[:, kc, col0:col0 + 128], hT[:, kc, :],
                                                          start=(kc == 0), stop=(kc == 7)),
                   reads=[wres, "hT"], writes=["bank%d" % b])
        evac(ps, "bank%d" % b)

    def cchain_a(G):
        for t in range(4):
            for kc in range(8):
                sc.add("pe", lambda e, kc=kc, t=t: e.matmul(
                    k.bank(7)[:, t * 8:(t + 1) * 8], hT[:, kc, t * 128:(t + 1) * 128], win[:, kc, 1536:1544],
                    start=(kc == 0), stop=(kc == 7)),
                    reads=["win0", "hT"], writes=["b7fg"])
        sc.add("dve", lambda e: e.tensor_tensor(out=zt, in0=k.bank(7)[:, 0:32], in1=bfb4.rearrange("p a b -> p (a b)"),
                                                op=ALU.add),
               reads=["b7fg", "bfb4"], writes=["zt"])
        sc.add("act", lambda e: e.activation(out=e1, in_=zt, func=AF.Exp, scale=-1.0), reads=["zt"], writes=["e1"])
        sc.add("act", lambda e: e.activation(out=sp, in_=e1, func=AF.Ln, bias=k.onecol), reads=["e1", "onecol"],
               writes=["sp"])

    def cchain_a2(G):
        sc.add("pe", lambda e: e.matmul(k.bank(7)[:, 32:64], k.negtri, sp, start=True, stop=True),
               reads=["sp", "cf"], writes=["b7w"])
        sc.add("dve", lambda e: e.tensor_copy(out=Wsb.rearrange("p a b -> p (a b)"), in_=k.bank(7)[:, 32:64]),
               reads=["b7w"], writes=["Wsb"])
        sc.add("dve", lambda e: e.tensor_copy(out=R[:, 1, :], in_=Wsb[:, 0, :]), reads=["Wsb"], writes=["R"])
        for t in range(2, 5):
            sc.add("dve", lambda e, t=t: e.tensor_tensor(out=R[:, t, :], in0=R[:, t - 1, :], in1=Wsb[:, t - 1, :],
                                                         op=ALU.add),
                   reads=["Wsb", "R"], writes=["R"])

    def cchain_b(G):
        cur = crefbc[G % 2]
        nxt = crefbc[(G + 1) % 2]
        cr, nr = "cref%d" % (G % 2), "cref%d" % ((G + 1) % 2)
        sc.add("pe", lambda e: e.matmul(k.bank(7)[:, 64:104], k.oneslast, R.rearrange("p a b -> p (a b)"),
                                        start=True, stop=True),
               reads=["R", "cf"], writes=["b7r"])
        psR = k.bank(7)[:, 64:104].rearrange("p (a b) -> p a b", a=5)
        sc.add("dve", lambda e: e.tensor_tensor(out=crel, in0=Wsb, in1=psR[:, 0:4, :], op=ALU.add),
               reads=["Wsb", "b7r"], writes=["crel"])
        sc.add("dve", lambda e: e.tensor_tensor(out=c_all[:, 4 * G:4 * G + 4, :], in0=crel,
                                                in1=cur.unsqueeze(1).to_broadcast([128, 4, 8]), op=ALU.add),
               reads=["crel", cr], writes=["c_all"])
        nk = 4 * G + 4
        sc.add("dve", lambda e: e.scalar_tensor_tensor(out=biasG[:, 0:nk, :], in0=c_all[:, 0:nk, :], scalar=-1.0,
                                                       in1=cur.unsqueeze(1).to_broadcast([128, nk, 8]),
                                                       op0=ALU.mult, op1=ALU.add),
               reads=["c_all", cr], writes=["biasG"])
        sc.add("dve", lambda e: e.tensor_tensor(out=nxt, in0=cur, in1=psR[:, 4, :], op=ALU.add),
               reads=[cr, "b7r"], writes=[nr])
        sc.add("dve", lambda e: e.tensor_copy(out=hib, in_=crel), reads=["crel"], writes=["hib"])
        sc.add("dve", lambda e: e.tensor_copy(out=X[:, :, 0:8], in_=hib), reads=["hib"], writes=["X"])
        sc.add("dve", lambda e: e.tensor_tensor(out=X[:, :, 8:16], in0=crel, in1=X[:, :, 0:8], op=ALU.subtract),
               reads=["crel", "X"], writes=["X"])

    def cchain_b2(G):
        b = pbank()
        for t in range(4):
            sc.add("pe", lambda e, t=t, b=b: e.transpose(k.bank(b)[0:16, t * 128:(t + 1) * 128], X[:, t, :], k.identf),
                   reads=["X", "cf"], writes=["bank%d" % b])
        sc.add("dve", lambda e, b=b: e.tensor_copy(out=dq[0:16, :], in_=k.bank(b)[0:16, :]),
               reads=["bank%d" % b], writes=["dq"])


    def projections(G):
        cchain_a(G)
        for p in range(4):
            proj_fm(p * 128, lambda ps, br, p=p: sc.add(
                "act", lambda e: e.activation(out=qT2[:, p, :], in_=ps, func=AF.Copy, scale=0.125),
                reads=[br], writes=["qT2"]))
        cchain_a2(G)
        for p in range(4):
            proj_fm(512 + p * 128, lambda ps, br, p=p: sc.add(
                "dve", lambda e: e.tensor_copy(out=kT2[:, p, G * 512:(G + 1) * 512], in_=ps),
                reads=[br], writes=["kT2"]))
        cchain_b(G)
        for t in range(4):
            b = pbank()
            ps = k.bank(b)
            for kc in range(8):
                sc.add("pe", lambda e, kc=kc, t=t, ps=ps: e.matmul(
                    ps, hT[:, kc, t * 128:(t + 1) * 128], win[:, kc, 1024:1536], start=(kc == 0), stop=(kc == 7)),
                    reads=["win0", "hT"], writes=["bank%d" % b])
            sc.add("act", lambda e, t=t, ps=ps: e.activation(
                out=vaug[:, 4 * G + t, :, 0:64], in_=ps.rearrange("p (h d) -> p h d", h=8), func=AF.Copy),
                reads=["bank%d" % b], writes=["vaug"])
        cchain_b2(G)
        if G > 0:
            sc.add("dve", lambda e: e.tensor_copy(out=uT[:, :, 0:16], in_=uT[:, :, 512:528]),
                   reads=["uT"], writes=["uT"])
        for g in range(4):
            proj_fm(1544 + g * 128, lambda ps, br, g=g: sc.add(
                "act", lambda e: e.activation(out=uT[:, g, 16:528], in_=ps, func=AF.Copy),
                reads=[br], writes=["uT"]))

    def pool_mixer(G):
        for g in range(4):
            src = uT[:, g, :]
            L = g + 1
            bufs = [pA, pB]
            names = ["pA", "pB"]
            prev, prev_r = src, "uT"
            lo = 0
            for lev in range(L):
                sh = 1 << lev
                lo2 = lo + sh
                dst, dst_r = bufs[lev % 2], names[lev % 2]
                sc.add("dve", lambda e, dst=dst, prev=prev, lo2=lo2, sh=sh: e.tensor_tensor(
                    out=dst[:, lo2:528], in0=prev[:, lo2:528], in1=prev[:, lo2 - sh:528 - sh], op=ALU.add),
                    reads=[prev_r], writes=[dst_r])
                prev, prev_r, lo = dst, dst_r, lo2
            w = 1 << L
            sc.add("dve", lambda e, g=g, prev=prev, src=src, w=w: e.scalar_tensor_tensor(
                out=poolT[:, g, :], in0=prev[:, 16:528], scalar=1.0 / w, in1=src[:, 16:528],
                op0=ALU.mult, op1=ALU.subtract),
                reads=[prev_r, "uT"], writes=["poolT"])
            if G == 0:
                sc.add("dve", lambda e, prev=prev, w=w: e.tensor_tensor(
                    out=tmpc[:, 0:w - 1], in0=prev[:, 16:16 + w - 1], in1=k.invc[:, 0:w - 1], op=ALU.mult),
                    reads=[prev_r, "cf"], writes=["tmpc"])
                sc.add("dve", lambda e, g=g, src=src, w=w: e.tensor_tensor(
                    out=poolT[:, g, 0:w - 1], in0=tmpc[:, 0:w - 1], in1=src[:, 16:16 + w - 1], op=ALU.subtract),
                    reads=["tmpc", "uT"], writes=["poolT"])
            b = 0
            ps = k.bank(b)
            sc.add("pe", lambda e, g=g, ps=ps: e.matmul(ps, poolw[:, g, :], poolT[:, g, :], start=True, stop=True),
                   reads=["poolw", "poolT"], writes=["bank%d" % b])
            sc.add("act", lambda e, g=g, ps=ps: e.activation(out=catT[:, 4 + g, :], in_=ps, func=AF.Copy,
                                                            scale=pscale[:, g:g + 1]),
                   reads=["bank%d" % b, "pscale"], writes=["catT"])

    def attention(G, bg):
        nkt = 4 * G + 4
        per_it = -(-len(bg) // max(1, 4 * nkt - 3))
        items = [(p, kt) for p in range(4) for kt in range(nkt)]

        def geom(idx):
            p, kt = items[idx]
            j = kt - 4 * G
            sbs = (1 + 2 * (idx % 2), 2 + 2 * (idx % 2))
            pts = (2 * (idx % 2), 2 * (idx % 2) + 1)
            return p, kt, j, 128 * max(j, 0), sbs, pts

        def stageA(idx):
            p, kt, j, c0, sbs, pts = geom(idx)
            for hb_ in range(2):
                pb = 64 * hb_
                psS = k.bank(sbs[hb_])
                sc.add("pe", lambda e, pb=pb, psS=psS: e.matmul(
                    psS[:, c0:512], kT2[pb:pb + 64, p, kt * 128:(kt + 1) * 128], qT2[pb:pb + 64, p, c0:512],
                    start=True, stop=False, tile_position=(pb, 0)),
                    reads=["kT2", "qT2"], writes=["bank%d" % sbs[hb_]])
            for hb_ in range(2):
                h = 2 * p + hb_
                psS = k.bank(sbs[hb_])
                sc.add("pe", lambda e, h=h, psS=psS: e.matmul(
                    psS[:, c0:512], k.sel[:, h, :], dq[:, c0:512], start=False, stop=True),
                    reads=["cf", "dq"], writes=["bank%d" % sbs[hb_]])
            if j >= 0:
                for hb_ in range(2):
                    psS = k.bank(sbs[hb_])
                    sc.add("dve", lambda e, psS=psS: e.tensor_tensor(
                        out=psS[:, c0:c0 + 128], in0=psS[:, c0:c0 + 128], in1=k.maskneg, op=ALU.add),
                        reads=["bank%d" % sbs[hb_], "cf"], writes=["bank%d" % sbs[hb_]])

        def stageB(idx):
            p, kt, j, c0, sbs, pts = geom(idx)
            for hb_ in range(2):
                h = 2 * p + hb_
                psS = k.bank(sbs[hb_])
                pt = pT[pts[hb_]]
                sc.add("act", lambda e, h=h, psS=psS, pt=pt: e.activation(
                    out=pt[:, c0:512], in_=psS[:, c0:512], func=AF.Exp, bias=biasG[:, kt, h:h + 1]),
                    reads=["bank%d" % sbs[hb_], "biasG"], writes=["pT%d" % pts[hb_]])

        def stageC(idx):
            p, kt, j, c0, sbs, pts = geom(idx)
            for hb_ in range(2):
                h = 2 * p + hb_
                psO = k.bank(5 + hb_)
                pt = pT[pts[hb_]]
                sc.add("pe", lambda e, h=h, psO=psO, pt=pt: e.matmul(
                    psO[0:65, c0:512], vaug[:, kt, h, :], pt[:, c0:512], start=(kt == 0), stop=(kt == nkt - 1)),
                    reads=["vaug", "pT%d" % pts[hb_]], writes=["bank%d" % (5 + hb_)])

        def use_act(p):
            return G <= 2 or p == 3

        def epi1(p):
            for hb_ in range(2):
                psO = k.bank(5 + hb_)
                sc.add("dve", lambda e, hb_=hb_, psO=psO: e.tensor_copy(out=oun[hb_][0:65, :], in_=psO[0:65, :]),
                       reads=["bank%d" % (5 + hb_)], writes=["oun%d" % hb_])
                if use_act(p):
                    sc.add("act", lambda e, hb_=hb_: e.activation(out=oun[hb_][64:65, :], in_=oun[hb_][64:65, :],
                                                                  func=AF.Ln),
                           reads=["oun%d" % hb_], writes=["oun%d" % hb_])
                else:
                    sc.add("dve", lambda e, hb_=hb_: e.reciprocal(out=oun[hb_][64:65, :], in_=oun[hb_][64:65, :]),
                           reads=["oun%d" % hb_], writes=["oun%d" % hb_])

        def epi2(p):
            for hb_ in range(2):
                pb = 64 * hb_
                sc.add("pe", lambda e, hb_=hb_: e.matmul(k.bank(7)[0:64, :], k.onesf[64:65, 0:64],
                                                          oun[hb_][64:65, :], start=True, stop=True),
                       reads=["oun%d" % hb_, "cf"], writes=["b7fg", "b7w", "b7r"])
                if use_act(p):
                    sc.add("act", lambda e: e.activation(out=k.bank(7)[0:64, :], in_=k.bank(7)[0:64, :],
                                                         func=AF.Exp, scale=-1.0),
                           reads=["b7fg", "b7w", "b7r"], writes=["b7fg", "b7w", "b7r"])
                sc.add("dve", lambda e, hb_=hb_, pb=pb: e.tensor_tensor(
                    out=catT[pb:pb + 64, p, :], in0=oun[hb_][0:64, :], in1=k.bank(7)[0:64, :], op=ALU.mult),
                    reads=["oun%d" % hb_, "b7fg", "b7w", "b7r"], writes=["catT"])

        pending = []
        stageA(0)
        for idx in range(len(items)):
            if idx + 1 < len(items):
                stageA(idx + 1)
            stageB(idx)
            stageC(idx)
            sc.replay(bg, per_it)
            for ep in pending:
                ep[1] -= 1
            while pending and pending[0][1] <= 0:
                epi2(pending.pop(0)[0])
            p, kt = items[idx]
            if kt == nkt - 1:
                epi1(p)
                pending.append([p, 3])
        while pending:
            epi2(pending.pop(0)[0])
        sc.replay(bg)

    def outproj(G):
        for t in range(4):
            i = 4 * G + t
            b0 = 1 + 2 * (t % 2)
            psD = k.bank2(b0)
            for half in range(2):
                for kc in range(8):
                    sc.add("pe", lambda e, kc=kc, half=half, t=t, psD=psD: e.matmul(
                        psD[:, half * 512:(half + 1) * 512], catT[:, kc, t * 128:(t + 1) * 128],
                        wout[:, kc, half * 512:(half + 1) * 512], start=(kc == 0), stop=(kc == 7)),
                        reads=["catT", "wout%d" % half], writes=["bank%d" % (b0 + half)])
            post_norm_residual(k, i, psD, ["bank%d" % b0, "bank%d" % (b0 + 1)], gpost, "gpost")

    normA(0)
    normB(0)
    for G in range(NCH):
        projections(G)

        def background(G=G):
            pool_mixer(G)
            if G + 1 < NCH:
                normA(G + 1)
                normB(G + 1)
        attention(G, sc.capture(background))
        outproj(G)
    sc.fence()
    k.hb = hb_save
    ar.reset(m0)


def ffn_phase(k, l):
    sc = k.sc
    ar = k.arena
    m0 = ar.mark()
    wup, _ = ar.alloc_at(WUP_OFF, [8, DFF], BF16)
    wdn = ar.alloc([32, D], BF16)
    gpre = ar.alloc([D], F32)
    gpost = ar.alloc([D], F32)
    hT = ar.alloc([8, 512], BF16)
    aT = ar.alloc([32, 512], BF16)
    r = [ar.alloc([512], BF16) for _ in range(2)]
    assert ar.off <= WUP_OFF, ("ffn phase collides with w_up home", ar.off, WUP_OFF)
    if getattr(k, "wup_loaded", None) != l:
        load_w_cols(k, wup, k.w_up[l], "wup", [(j * 512, (j + 1) * 512) for j in range(8)])
    for j in range(8):
        sc.add("pool", lambda e, j=j: e.dma_start(
            out=wdn[:, 4 * j:4 * j + 4, :],
            in_=k.w_down[l, 512 * j:512 * (j + 1), :].rearrange("(f p) n -> p f n", p=128)),
            writes=["wdn%d" % j], dsem="wdn%d" % j)
    load_bcast(k, gpre, k.g_ffn_pre[l:l + 1, :], "gpre", "gpre")
    load_bcast(k, gpost, k.g_ffn_post[l:l + 1, :], "gpost", "gpost")

    def normT(c):
        pend = None
        for t in range(4):
            cur = (t,) + norm_tile(k, 4 * c + t, gpre, "gpre")
            if pend is not None:
                transpose_tile(k, 4 * c + pend[0], pend[1], pend[2], hT, "hT", pend[0] * 128, (0, 1))
            pend = cur
        transpose_tile(k, 4 * c + pend[0], pend[1], pend[2], hT, "hT", pend[0] * 128, (0, 1))

    def up(c):
        for f in range(32):
            ps = k.bank(2 + f % 2)
            pr = "bank%d" % (2 + f % 2)
            for kc in range(8):
                sc.add("pe", lambda e, kc=kc, f=f, ps=ps: e.matmul(
                    ps, wup[:, kc, f * 128:(f + 1) * 128], hT[:, kc, :], start=(kc == 0), stop=(kc == 7)),
                    reads=["wup%d" % (f // 4), "hT"], writes=[pr])
            rr = r[f % 2]
            rres = "r%d" % (f % 2)
            sc.add("act", lambda e, ps=ps, rr=rr: e.activation(out=rr, in_=ps, func=AF.Relu),
                   reads=[pr], writes=[rres])
            sc.add("dve", lambda e, f=f, rr=rr: e.tensor_tensor(out=aT[:, f, :], in0=rr, in1=rr, op=ALU.mult),
                   reads=[rres], writes=["aT"])

    def down(c, bg):
        per = -(-len(bg) // 200)
        for t in range(4):
            i = 4 * c + t
            b0 = 4 + 2 * (i % 2)
            psD = k.bank2(b0)
            for half in range(2):
                for f in range(32):
                    sc.add("pe", lambda e, f=f, half=half, psD=psD, t=t: e.matmul(
                        psD[:, half * 512:(half + 1) * 512], aT[:, f, t * 128:(t + 1) * 128],
                        wdn[:, f, half * 512:(half + 1) * 512], start=(f == 0), stop=(f == 31)),
                        reads=["aT", "wdn%d" % (f // 4)], writes=["bank%d" % (b0 + half)])
                    sc.replay(bg, per)
            post_norm_residual(k, i, psD, ["bank%d" % b0, "bank%d" % (b0 + 1)], gpost, "gpost")
        sc.replay(bg)

    normT(0)
    for c in range(NCH):
        up(c)
        down(c, sc.capture(lambda c=c: normT(c + 1)) if c + 1 < NCH else [])
    sc.fence()
    ar.reset(m0)


def build_program(phases=("mix", "cross", "ffn"), layers=range(DEPTH)):
    nc = bass.Bass("TRN2", target_bir_lowering=False)
    k = K()
    k.nc = nc

    def din(name, shape):
        return nc.dram_tensor(name, list(shape), F32, kind="ExternalInput").ap()

    k.x = din("x", [S, D])
    k.mem = din("mem", [MEM, D])
    k.g_mix_pre = din("g_mix_pre", [DEPTH, D])
    k.w_in = din("w_in", [DEPTH, D, IN_COLS])
    k.b_forget = din("b_forget", [DEPTH, 8])
    k.pool_w = din("pool_w", [DEPTH, 4, 128, 128])
    k.pool_scale = din("pool_scale", [DEPTH, 512])
    k.w_out = din("w_out", [DEPTH, D, D])
    k.g_mix_post = din("g_mix_post", [DEPTH, D])
    k.g_x_pre = din("g_x_pre", [DEPTH, D])
    k.g_mem = din("g_mem", [DEPTH, D])
    k.wq_x = din("wq_x", [DEPTH, D, D])
    k.wkv_x = din("wkv_x", [DEPTH, D, 2 * D])
    k.wo_x = din("wo_x", [DEPTH, D, D])
    k.g_x_post = din("g_x_post", [DEPTH, D])
    k.g_ffn_pre = din("g_ffn_pre", [DEPTH, D])
    k.w_up = din("w_up", [DEPTH, D, DFF])
    k.w_down = din("w_down", [DEPTH, DFF, D])
    k.g_ffn_post = din("g_ffn_post", [DEPTH, D])
    k.consts = din("consts", [128, NCONST])
    k.pool_scale_t = din("pool_scale_t", [DEPTH, 128, 4])
    k.xs = nc.dram_tensor("out", [S, D], F32, kind="ExternalOutput").ap()

    sc = Sched(nc)
    k.sc = sc
    k.prefetch_wup = ("cross" in phases and "ffn" in phases)
    k.wup_loaded = None
    ar = Arena(nc, 206 * 1024)
    k.arena = ar
    k.ident = ar.alloc([128], BF16)
    k.ones = ar.alloc([128], BF16)
    k.onecol = ar.alloc([1], F32)
    k.epsb = ar.alloc([1], F32)
    k.stat = ar.alloc([4, 4], F32)
    k.xin = [ar.alloc([D], F32) for _ in range(2)]
    k.hb = [ar.alloc([D], BF16) for _ in range(2)]
    k.yt = [ar.alloc([D], F32) for _ in range(2)]
    k.ps = nc.alloc_psum_tensor("ps", [128, 8 * 512], F32)
    k.bank = lambda b: k.ps[:, b * 512:(b + 1) * 512]
    k.bank2 = lambda b: k.ps[:, b * 512:(b + 2) * 512]
    k.bankT = lambda b: k.ps[:, b * 512:(b + 1) * 512].bitcast(BF16).rearrange("p (a b) -> p a b", a=8)

    sc.add("pool", lambda e: e.dma_start(out=k.ident, in_=k.consts[:, NCONST - 1024 - 128:NCONST - 1024]),
           writes=["ident"], dsem="ident")
    sc.add("dve", lambda e: e.memset(k.epsb, EPS), writes=["epsb"])
    sc.add("dve", lambda e: e.memset(k.ones, 1.0), writes=["ones"])
    sc.add("dve", lambda e: e.memset(k.onecol, 1.0), writes=["onecol"])

    for c in range(NCH):
        sc.add("sp", lambda e, c=c: e.dma_start(out=k.xs[c * 512:(c + 1) * 512, :], in_=k.x[c * 512:(c + 1) * 512, :]),
               writes=[("xs", 4 * c + t) for t in range(4)], dsem="xcopy%d" % c)
    for l in layers:
        if "mix" in phases:
            mix_phase(k, l)
        if "cross" in phases:
            cross_phase(k, l)
        if "ffn" in phases:
            ffn_phase(k, l)
    final = sorted(set(o.dsem for o in sc.ops if o.isdma), key=str)
    sc.emit(final_dsems=final)
    return nc


INPUT_NAMES = ["x", "mem", "g_mix_pre", "w_in", "b_forget", "pool_w", "pool_scale", "w_out", "g_mix_post",
               "g_x_pre", "g_mem", "wq_x", "wkv_x", "wo_x", "g_x_post", "g_ffn_pre", "w_up", "w_down",
               "g_ffn_post"]


def make_consts():
    c = np.zeros((128, NCONST), np.float32)
    eye = np.eye(128, dtype=np.float32)
    idx = np.arange(128)
    c[:, 0:128] = eye
    c[:, 128:256] = -(idx[:, None] <= idx[None, :]).astype(np.float32)
    c[127, 256:384] = 1.0
    c[:, 384:512] = np.where(idx[:, None] <= idx[None, :], 0.0, -30000.0)
    c[:, 512:528] = 1.0 / (np.arange(16) + 1.0)[None, :]
    c[:, 528:592] = 1.0
    c[:, 592:720] = eye
    selbase = 720
    for h in range(8):
        c[h, selbase + h * 128:selbase + (h + 1) * 128] = 1.0
        c[8 + h, selbase + h * 128:selbase + (h + 1) * 128] = 1.0
    return c


def kernel(**inputs):
    nc = build_program()
    consts = make_consts()
    in_maps = []
    for b in range(N_CORES):
        m = {}
        for n in INPUT_NAMES:
            a = np.asarray(inputs[n], dtype=np.float32)
            if n in ("x", "mem"):
                a = a[b]
            m[n] = np.ascontiguousarray(a)
        m["consts"] = consts
        m["pool_scale_t"] = np.ascontiguousarray(
            np.asarray(inputs["pool_scale"], dtype=np.float32).reshape(DEPTH, 4, 128).transpose(0, 2, 1))
        in_maps.append(m)
    res = run_bass_kernel_spmd(nc, in_maps, core_ids=list(range(N_CORES)))
    return np.stack([np.asarray(r["out"], dtype=np.float32) for r in res.results], axis=0)
```

```python
import numpy as np
import concourse.bass as bass
import concourse.mybir as mybir
from concourse.bass_utils import run_bass_kernel_spmd

F32 = mybir.dt.float32
BF16 = mybir.dt.bfloat16
U8 = mybir.dt.uint8
AF = mybir.ActivationFunctionType
ALU = mybir.AluOpType

D = 1024
S = 4096
DEPTH = 4
NT = S // 128
NCH = S // 512
DFF = 4096
MEM = 256
EPS = 1e-6
IN_COLS = 2056
N_CORES = 8
NCONST = 592 + 128 + 1024


class Op:
    __slots__ = ("eng", "fn", "deps", "marked", "count", "dsem", "idx", "isdma")


class Sched:
    ENGS = ("pe", "act", "dve", "pool", "sp")

    def __init__(self, nc):
        self.nc = nc
        self.ops = []
        self.last_w = {}
        self.readers = {}
        self.dsem_counts = {}
        self.fence_deps = {e: [] for e in self.ENGS}
        self.capturing = None

    def capture(self, fn):
        assert self.capturing is None
        self.capturing = []
        fn()
        lst, self.capturing = self.capturing, None
        return lst

    def replay(self, lst, n=None):
        n = len(lst) if n is None else min(n, len(lst))
        for _ in range(n):
            self.add(*lst.pop(0))

    def add(self, eng, fn, reads=(), writes=(), dsem=None):
        if self.capturing is not None:
            self.capturing.append((eng, fn, tuple(reads), tuple(writes), dsem))
            return None
        op = Op()
        op.eng = eng
        op.fn = fn
        op.dsem = dsem
        op.isdma = dsem is not None
        op.idx = len(self.ops)
        op.marked = False
        op.count = 0
        deps = list(self.fence_deps[eng])
        self.fence_deps[eng] = []
        for r in reads:
            w = self.last_w.get(r)
            if w is not None:
                deps.append(w)
        for r in writes:
            w = self.last_w.get(r)
            if w is not None:
                deps.append(w)
            deps.extend(self.readers.get(r, ()))
        for r in reads:
            self.readers.setdefault(r, []).append(op)
        for r in writes:
            self.last_w[r] = op
            self.readers[r] = []
        best = {}
        for d in deps:
            if d is op:
                continue
            key = ("d", d.dsem) if d.isdma else ("e", d.eng)
            if key not in best or best[key].idx < d.idx:
                best[key] = d
        if eng == "pe" and not op.isdma:
            best.pop(("e", "pe"), None)
        op.deps = list(best.values())
        for d in op.deps:
            d.marked = True
        if op.isdma:
            self.dsem_counts[dsem] = self.dsem_counts.get(dsem, 0) + 16
            op.count = self.dsem_counts[dsem]
        self.ops.append(op)
        return op

    def fence(self):
        last = {}
        for op in self.ops:
            key = ("d", op.dsem) if op.isdma else ("e", op.eng)
            last[key] = op
        for e in self.ENGS:
            self.fence_deps[e] = list(last.values())

    def emit(self, final_dsems=()):
        nc = self.nc
        cnt = {e: 0 for e in self.ENGS}
        for op in self.ops:
            if not op.isdma and op.marked:
                cnt[op.eng] += 1
                op.count = cnt[op.eng]
        esem = {e: nc.alloc_semaphore("s_" + e) for e in self.ENGS if e != "sp"}
        dsem = {k: nc.alloc_semaphore("d_%d" % i) for i, k in enumerate(self.dsem_counts)}
        per_eng = {e: [o for o in self.ops if o.eng == e] for e in self.ENGS}

        def run(e, handle):
            waited = {}
            for op in per_eng[e]:
                for d in op.deps:
                    sem = dsem[d.dsem] if d.isdma else esem[d.eng]
                    if waited.get(sem, 0) < d.count:
                        handle.wait_ge(sem, d.count)
                        waited[sem] = d.count
                ins = op.fn(handle)
                if op.isdma:
                    ins.then_inc(dsem[op.dsem], 16)
                elif op.marked:
                    ins.then_inc(esem[e], 1)
            if e == "sp":
                for k in final_dsems:
                    handle.wait_ge(dsem[k], self.dsem_counts[k])

        with nc.Block() as block:
            @block.sync
            def _(h):
                run("sp", h)

            @block.tensor
            def _(h):
                run("pe", h)

            @block.vector
            def _(h):
                run("dve", h)

            @block.scalar
            def _(h):
                run("act", h)

            @block.gpsimd
            def _(h):
                run("pool", h)


class Arena:
    def __init__(self, nc, nbytes):
        self.t = nc.alloc_sbuf_tensor("arena", [128, nbytes], U8)
        self.nbytes = nbytes
        self.off = 0

    def mark(self):
        return self.off

    def alloc_at(self, off, shape, dtype):
        save = self.off
        self.off = off
        ap = self.alloc(shape, dtype)
        end = self.off
        self.off = save
        return ap, end

    def reset(self, off):
        self.off = off

    def alloc(self, shape, dtype):
        esz = 4 if dtype == F32 else 2
        n = 1
        for s in shape:
            n *= s
        nb = (n * esz + 63) // 64 * 64
        assert self.off + nb <= self.nbytes, ("SBUF arena overflow", self.off + nb, self.nbytes)
        ap = self.t[:, self.off:self.off + n * esz].bitcast(dtype)
        self.off += nb
        if len(shape) == 2:
            ap = ap.rearrange("p (a b) -> p a b", a=shape[0])
        elif len(shape) == 3:
            ap = ap.rearrange("p (a b c) -> p a b c", a=shape[0], b=shape[1])
        return ap


class K:
    pass


def rms_stats(k, src_ap, src_res, slot, junk, junk_res):
    sc = k.sc
    ss = k.stat[:, slot, 0:1]
    ln = k.stat[:, slot, 1:2]
    rs = k.stat[:, slot, 2:3]
    rn = "stat%d" % slot
    sc.add("act", lambda e: e.activation(out=junk, in_=src_ap, func=AF.Square, accum_out=ss),
           reads=(list(src_res) if isinstance(src_res, list) else [src_res]), writes=[rn, junk_res])
    sc.add("act", lambda e: e.activation(out=ln, in_=ss, func=AF.Ln, scale=1.0 / D, bias=k.epsb),
           reads=[rn, "epsb"], writes=[rn])
    sc.add("act", lambda e: e.activation(out=rs, in_=ln, func=AF.Exp, scale=-0.5),
           reads=[rn], writes=[rn])
    return rs, rn


def load_bcast(k, dst, src_row, res, dsem):
    k.sc.add("sp", lambda e: e.dma_start(out=dst, in_=src_row.partition_broadcast(128)),
             writes=[res], dsem=dsem)


def norm_tile(k, i, gpre, gres, src=None, src_res=None):
    sc = k.sc
    s2 = i % 2
    xin = k.xin[s2]
    xr = "xin%d" % s2
    if src is None:
        src = k.xs[i * 128:(i + 1) * 128, :]
        src_res = ("xs", i)
    sc.add("sp", lambda e: e.dma_start(out=xin, in_=src),
           reads=[src_res], writes=[xr], dsem=xr)
    sh = i % len(k.hb)
    hb = k.hb[sh]
    hr = "hb%d" % sh
    rs, rn = rms_stats(k, xin, xr, s2, hb, hr)
    sc.add("dve", lambda e: e.scalar_tensor_tensor(out=hb, in0=xin, scalar=rs, in1=gpre,
                                                   op0=ALU.mult, op1=ALU.mult),
           reads=[xr, rn, gres], writes=[hr])
    return hb, hr


def transpose_tile(k, i, hb, hr, hT, hT_res, tcol, psT_banks):
    sc = k.sc
    b = psT_banks[i % len(psT_banks)]
    psT = k.bankT(b)
    pr = "bank%d" % b
    for kc in range(8):
        sc.add("pe", lambda e, kc=kc: e.transpose(psT[:, kc, :], hb[:, kc * 128:(kc + 1) * 128], k.ident),
               reads=[hr, "ident"], writes=[pr])
    sc.add("dve", lambda e: e.tensor_copy(out=hT[:, :, tcol:tcol + 128], in_=psT),
           reads=[pr], writes=[hT_res])


def norm_transpose_tile(k, i, gpre, gres, hT, hT_res, tcol, psT_banks, src=None, src_res=None):
    hb, hr = norm_tile(k, i, gpre, gres, src, src_res)
    transpose_tile(k, i, hb, hr, hT, hT_res, tcol, psT_banks)


def post_norm_residual(k, i, psD, psD_res, gpost, gres):
    sc = k.sc
    s2 = i % 2
    yt = k.yt[s2]
    yr = "yt%d" % s2
    rs, rn = rms_stats(k, psD, psD_res, 2 + s2, yt.bitcast(BF16)[:, 0:D], yr)
    sc.add("dve", lambda e: e.scalar_tensor_tensor(out=yt, in0=psD, scalar=rs, in1=gpost,
                                                   op0=ALU.mult, op1=ALU.mult),
           reads=list(psD_res) + [rn, gres], writes=[yr])
    sc.add("pool", lambda e: e.dma_start(out=k.xs[i * 128:(i + 1) * 128, :], in_=yt, accum_op=ALU.add),
           reads=[yr], writes=[("xs", i)], dsem=yr)


def load_w(k, dst, src2d, res, nsplit=8):
    per = 8 // nsplit
    for j in range(nsplit):
        k.sc.add("pool", lambda e, j=j: e.dma_start(
            out=dst[:, j * per:(j + 1) * per, :],
            in_=src2d[j * per * 128:(j + 1) * per * 128, :].rearrange("(c p) n -> p c n", p=128)),
            writes=[res], dsem=res)


def load_w_cols(k, dst, src2d, res, blocks, extra_writes=()):
    for j, (c0, c1) in enumerate(blocks):
        k.sc.add("pool", lambda e, c0=c0, c1=c1: e.dma_start(
            out=dst[:, :, c0:c1], in_=src2d[:, c0:c1].rearrange("(c p) n -> p c n", p=128)),
            writes=["%s%d" % (res, j)] + list(extra_writes), dsem="%s%d" % (res, j))


WUP_OFF = 206 * 1024 - 8 * DFF * 2


def cross_phase(k, l):
    sc = k.sc
    ar = k.arena
    m0 = ar.mark()
    wq = ar.alloc([8, D], BF16)
    wo = ar.alloc([8, D], BF16)
    wkv, _ = ar.alloc_at(WUP_OFF, [8, 2 * D], BF16)
    gpre = ar.alloc([D], F32)
    gpost = ar.alloc([D], F32)
    gmem = ar.alloc([D], F32)
    mT = ar.alloc([8, MEM], BF16)
    kxT = ar.alloc([8, MEM], BF16)
    vx = ar.alloc([2, D], BF16)
    hT = ar.alloc([8, 512], BF16)
    qT = ar.alloc([8, 512], BF16)
    oT = ar.alloc([8, 512], BF16)
    pT = [ar.alloc([512], BF16) for _ in range(4)]
    rec = ar.alloc([512], F32)
    load_w_cols(k, wkv, k.wkv_x[l], "wkv", [(j * 512, (j + 1) * 512) for j in range(4)])
    load_w_cols(k, wq, k.wq_x[l], "wq", [(0, 512), (512, 1024)])
    load_w_cols(k, wo, k.wo_x[l], "wo", [(0, 512), (512, 1024)])
    load_bcast(k, gmem, k.g_mem[l:l + 1, :], "gmem", "gmem")
    load_bcast(k, gpre, k.g_x_pre[l:l + 1, :], "gpre", "gpre")
    load_bcast(k, gpost, k.g_x_post[l:l + 1, :], "gpost", "gpost")
    def normT(c):
        pend = None
        for t in range(4):
            cur = (t,) + norm_tile(k, 4 * c + t, gpre, "gpre")
            if pend is not None:
                transpose_tile(k, 4 * c + pend[0], pend[1], pend[2], hT, "hT", pend[0] * 128, (0,))
            pend = cur
        transpose_tile(k, 4 * c + pend[0], pend[1], pend[2], hT, "hT", pend[0] * 128, (0,))

    normT(0)
    for mt in range(2):
        norm_transpose_tile(k, mt, gmem, "gmem", mT, "mT", mt * 128, (0,),
                            src=k.mem[mt * 128:(mt + 1) * 128, :], src_res="mem")
    for m in range(8):
        b = 1 + m % 2
        ps = k.bank(b)[:, 0:MEM]
        for kc in range(8):
            sc.add("pe", lambda e, kc=kc, m=m, ps=ps: e.matmul(
                ps, wkv[:, kc, m * 128:(m + 1) * 128], mT[:, kc, :], start=(kc == 0), stop=(kc == 7)),
                reads=["wkv%d" % (m // 4), "mT"], writes=["bank%d" % b])
        sc.add("dve", lambda e, m=m, ps=ps: e.tensor_copy(out=kxT[:, m, :], in_=ps),
               reads=["bank%d" % b], writes=["kxT"])
    for mt in range(2):
        for half in range(2):
            b = 1 + half
            ps = k.bank(b)
            for kc in range(8):
                sc.add("pe", lambda e, kc=kc, mt=mt, half=half, ps=ps: e.matmul(
                    ps, mT[:, kc, mt * 128:(mt + 1) * 128], wkv[:, kc, D + half * 512:D + (half + 1) * 512],
                    start=(kc == 0), stop=(kc == 7)),
                    reads=["wkv%d" % (2 + half), "mT"], writes=["bank%d" % b])
            sc.add("act", lambda e, mt=mt, half=half, ps=ps: e.activation(
                out=vx[:, mt, half * 512:(half + 1) * 512], in_=ps, func=AF.Copy),
                reads=["bank%d" % b], writes=["vx"])

    def attn(c):
        for m in range(8):
            b = 1 + m % 2
            ps = k.bank(b)
            for kc in range(8):
                sc.add("pe", lambda e, kc=kc, m=m, ps=ps: e.matmul(
                    ps, wq[:, kc, m * 128:(m + 1) * 128], hT[:, kc, :], start=(kc == 0), stop=(kc == 7)),
                    reads=["wq%d" % (m // 4), "hT"], writes=["bank%d" % b])
            sc.add("act", lambda e, m=m, ps=ps: e.activation(out=qT[:, m, :], in_=ps, func=AF.Copy, scale=1.0 / 16.0),
                   reads=["bank%d" % b], writes=["qT"])

        def stA(h):
            for mt in range(2):
                b = 1 + 2 * (h % 2) + mt
                ps = k.bank(b)
                for j in range(2):
                    sc.add("pe", lambda e, j=j, mt=mt, ps=ps: e.matmul(
                        ps, kxT[:, 2 * h + j, mt * 128:(mt + 1) * 128], qT[:, 2 * h + j, :],
                        start=(j == 0), stop=(j == 1)),
                        reads=["kxT", "qT"], writes=["bank%d" % b])

        def stB(h):
            for mt in range(2):
                b = 1 + 2 * (h % 2) + mt
                sl = 2 * (h % 2) + mt
                sc.add("act", lambda e, b=b, sl=sl: e.activation(out=pT[sl], in_=k.bank(b), func=AF.Exp),
                       reads=["bank%d" % b], writes=["pT%d" % sl])

        def stC(h):
            sls = [2 * (h % 2) + mt for mt in range(2)]
            for mt in range(2):
                sc.add("pe", lambda e, mt=mt: e.matmul(k.bank(5), k.ones, pT[sls[mt]], start=(mt == 0), stop=(mt == 1)),
                       reads=["ones", "pT%d" % sls[mt]], writes=["bank5"])
            for dj in range(2):
                for mt in range(2):
                    sc.add("pe", lambda e, mt=mt, dj=dj: e.matmul(
                        k.bank(6 + dj), vx[:, mt, h * 256 + dj * 128:h * 256 + (dj + 1) * 128], pT[sls[mt]],
                        start=(mt == 0), stop=(mt == 1)),
                        reads=["vx", "pT%d" % sls[mt]], writes=["bank%d" % (6 + dj)])
            sc.add("act", lambda e: e.activation(out=rec, in_=k.bank(5), func=AF.Ln), reads=["bank5"], writes=["rec"])
            sc.add("act", lambda e: e.activation(out=rec, in_=rec, func=AF.Exp, scale=-1.0), reads=["rec"], writes=["rec"])
            for dj in range(2):
                sc.add("dve", lambda e, dj=dj: e.tensor_tensor(
                    out=oT[:, 2 * h + dj, :], in0=k.bank(6 + dj), in1=rec, op=ALU.mult),
                    reads=["bank%d" % (6 + dj), "rec"], writes=["oT"])

        stA(0)
        for h in range(4):
            if h + 1 < 4:
                stA(h + 1)
            stB(h)
            stC(h)

    def oproj(c, bg):
        per = -(-len(bg) // 50)
        for t in range(4):
            i = 4 * c + t
            b0 = 1 + 2 * (t % 2)
            psD = k.bank2(b0)
            for half in range(2):
                for kc in range(8):
                    sc.add("pe", lambda e, kc=kc, half=half, t=t, psD=psD: e.matmul(
                        psD[:, half * 512:(half + 1) * 512], oT[:, kc, t * 128:(t + 1) * 128],
                        wo[:, kc, half * 512:(half + 1) * 512], start=(kc == 0), stop=(kc == 7)),
                        reads=["oT", "wo%d" % half], writes=["bank%d" % (b0 + half)])
                    sc.replay(bg, per)
            post_norm_residual(k, i, psD, ["bank%d" % b0, "bank%d" % (b0 + 1)], gpost, "gpost")
        sc.replay(bg)

    if k.prefetch_wup:
        wup, _ = ar.alloc_at(WUP_OFF, [8, DFF], BF16)
        assert ar.off <= WUP_OFF, ("cross phase collides with w_up home", ar.off, WUP_OFF)
        load_w_cols(k, wup, k.w_up[l], "wup", [(j * 512, (j + 1) * 512) for j in range(8)],
                    extra_writes=["wkv%d" % j for j in range(4)])
        k.wup_loaded = l
    for c in range(NCH):
        attn(c)
        oproj(c, sc.capture(lambda c=c: normT(c + 1)) if c + 1 < NCH else [])
    sc.fence()
    ar.reset(m0)


def mix_phase(k, l):
    sc = k.sc
    ar = k.arena
    m0 = ar.mark()
    hb_save = k.hb
    k.hb = k.hb + [ar.alloc([D], BF16) for _ in range(2)]
    win = ar.alloc([8, IN_COLS], BF16)
    wout = ar.alloc([8, D], BF16)
    poolw = ar.alloc([4, 128], BF16)
    pscale = ar.alloc([4], F32)
    gpre = ar.alloc([D], F32)
    gpost = ar.alloc([D], F32)
    bfb4 = ar.alloc([4, 8], F32)
    kT2 = ar.alloc([4, S], BF16)
    vaug = ar.alloc([NT, 8, 65], BF16)
    k.cf = ar.alloc([NCONST - 128 - 1024], F32)
    k.identf = k.cf[:, 0:128]
    k.negtri = k.cf[:, 128:256]
    k.oneslast = k.cf[:, 256:384]
    k.maskneg = k.cf[:, 384:512]
    k.invc = k.cf[:, 512:528]
    k.onesf = k.cf[:, 528:592]
    k.sel = ar.alloc([8, 128], BF16)
    sc.add("dve", lambda e: e.memset(vaug[:, :, :, 64:65], 1.0), writes=["vaug"])
    sc.add("sp", lambda e: e.dma_start(out=k.cf, in_=k.consts[:, 0:NCONST - 1024 - 128]), writes=["cf"], dsem="cf")
    sc.add("dve", lambda e: e.memset(k.sel, 0.0), writes=["cf"])
    sc.add("pool", lambda e: e.dma_start(
        out=k.sel[0:16], in_=k.consts[0:16, NCONST - 1024:NCONST].rearrange("p (h n) -> p h n", h=8)),
        writes=["cf"], dsem="sel")
    c_all = ar.alloc([NT, 8], F32)
    biasG = ar.alloc([NT, 8], F32)
    crefbc = [ar.alloc([8], F32) for _ in range(2)]
    hT = ar.alloc([8, 512], BF16)
    qT2 = ar.alloc([4, 512], BF16)
    dq = ar.alloc([512], BF16)
    catT = ar.alloc([8, 512], BF16)
    uT = ar.alloc([4, 528], F32)
    pA = ar.alloc([528], F32)
    pB = ar.alloc([528], F32)
    tmpc = ar.alloc([16], F32)
    poolT = ar.alloc([4, 512], BF16)
    pT = [ar.alloc([512], BF16) for _ in range(4)]
    oun = [ar.alloc([512], F32) for _ in range(2)]
    zt = ar.alloc([32], F32)
    e1 = ar.alloc([32], F32)
    sp = ar.alloc([32], F32)
    Wsb = ar.alloc([4, 8], F32)
    R = ar.alloc([5, 8], F32)
    crel = ar.alloc([4, 8], F32)
    hib = ar.alloc([4, 8], BF16)
    X = ar.alloc([4, 16], F32)

    load_w_cols(k, win, k.w_in[l], "win", [(1024, 1544), (0, 512), (512, 1024), (1544, 2056)])
    load_w_cols(k, wout, k.w_out[l], "wout", [(0, 512), (512, 1024)])
    sc.add("pool", lambda e: e.dma_start(out=poolw, in_=k.pool_w[l].rearrange("g c d -> c g d")),
           writes=["poolw"], dsem="poolw")
    sc.add("sp", lambda e: e.dma_start(out=pscale, in_=k.pool_scale_t[l]), writes=["pscale"], dsem="pscale")
    load_bcast(k, gpre, k.g_mix_pre[l:l + 1, :], "gpre", "gpre")
    load_bcast(k, gpost, k.g_mix_post[l:l + 1, :], "gpost", "gpost")
    for t in range(4):
        sc.add("sp", lambda e, t=t: e.dma_start(out=bfb4[:, t, :], in_=k.b_forget[l:l + 1, :].partition_broadcast(128)),
               writes=["bfb4"], dsem="bfb4")
    sc.add("dve", lambda e: e.memset(uT[:, :, 0:16], 0.0), writes=["uT"])
    sc.add("dve", lambda e: e.memset(dq, 0.0), writes=["dq"])
    sc.add("dve", lambda e: e.memset(crefbc[0], 0.0), writes=["cref0"])
    sc.add("dve", lambda e: e.memset(R[:, 0, :], 0.0), writes=["R"])

    normed = {}

    def normA(c):
        for t in range(4):
            normed[4 * c + t] = norm_tile(k, 4 * c + t, gpre, "gpre")

    def normB(c):
        for t in range(4):
            hb, hr = normed.pop(4 * c + t)
            transpose_tile(k, 4 * c + t, hb, hr, hT, "hT", t * 128, (0,))

    rot = [0]

    def pbank():
        rot[0] ^= 1
        return 1 + rot[0]

    def proj_fm(col0, evac):
        b = pbank()
        ps = k.bank(b)
        wres = "win1" if col0 < 512 else ("win2" if col0 < 1024 else "win3")
        for kc in range(8):
            sc.add("pe", lambda e, kc=kc, ps=ps: e.matmul(ps, win[:, kc, col0:col0 + 128], hT[:, kc, :],
                                                          start=(kc == 0), stop=(kc == 7)),
                   reads=[wres, "hT"], writes=["bank%d" % b])
        evac(ps, "bank%d" % b)

    def cchain_a(G):
        for t in range(4):
            for kc in range(8):
                sc.add("pe", lambda e, kc=kc, t=t: e.matmul(
                    k.bank(7)[:, t * 8:(t + 1) * 8], hT[:, kc, t * 128:(t + 1) * 128], win[:, kc, 1536:1544],
                    start=(kc == 0), stop=(kc == 7)),
                    reads=["win0", "hT"], writes=["b7fg"])
        sc.add("dve", lambda e: e.tensor_tensor(out=zt, in0=k.bank(7)[:, 0:32], in1=bfb4.rearrange("p a b -> p (a b)"),
                                                op=ALU.add),
               reads=["b7fg", "bfb4"], writes=["zt"])
        sc.add("act", lambda e: e.activation(out=e1, in_=zt, func=AF.Exp, scale=-1.0), reads=["zt"], writes=["e1"])
        sc.add("act", lambda e: e.activation(out=sp, in_=e1, func=AF.Ln, bias=k.onecol), reads=["e1", "onecol"],
               writes=["sp"])

    def cchain_a2(G):
        sc.add("pe", lambda e: e.matmul(k.bank(7)[:, 32:64], k.negtri, sp, start=True, stop=True),
               reads=["sp", "cf"], writes=["b7w"])
        sc.add("dve", lambda e: e.tensor_copy(out=Wsb.rearrange("p a b -> p (a b)"), in_=k.bank(7)[:, 32:64]),
               reads=["b7w"], writes=["Wsb"])
        sc.add("dve", lambda e: e.tensor_copy(out=R[:, 1, :], in_=Wsb[:, 0, :]), reads=["Wsb"], writes=["R"])
        for t in range(2, 5):
            sc.add("dve", lambda e, t=t: e.tensor_tensor(out=R[:, t, :], in0=R[:, t - 1, :], in1=Wsb[:, t - 1, :],
                                                         op=ALU.add),
                   reads=["Wsb", "R"], writes=["R"])

    def cchain_b(G):
        cur = crefbc[G % 2]
        nxt = crefbc[(G + 1) % 2]
        cr, nr = "cref%d" % (G % 2), "cref%d" % ((G + 1) % 2)
        sc.add("pe", lambda e: e.matmul(k.bank(7)[:, 64:104], k.oneslast, R.rearrange("p a b -> p (a b)"),
                                        start=True, stop=True),
               reads=["R", "cf"], writes=["b7r"])
        psR = k.bank(7)[:, 64:104].rearrange("p (a b) -> p a b", a=5)
        sc.add("dve", lambda e: e.tensor_tensor(out=crel, in0=Wsb, in1=psR[:, 0:4, :], op=ALU.add),
               reads=["Wsb", "b7r"], writes=["crel"])
        sc.add("dve", lambda e: e.tensor_tensor(out=c_all[:, 4 * G:4 * G + 4, :], in0=crel,
                                                in1=cur.unsqueeze(1).to_broadcast([128, 4, 8]), op=ALU.add),
               reads=["crel", cr], writes=["c_all"])
        nk = 4 * G + 4
        sc.add("dve", lambda e: e.scalar_tensor_tensor(out=biasG[:, 0:nk, :], in0=c_all[:, 0:nk, :], scalar=-1.0,
                                                       in1=cur.unsqueeze(1).to_broadcast([128, nk, 8]),
                                                       op0=ALU.mult, op1=ALU.add),
               reads=["c_all", cr], writes=["biasG"])
        sc.add("dve", lambda e: e.tensor_tensor(out=nxt, in0=cur, in1=psR[:, 4, :], op=ALU.add),
               reads=[cr, "b7r"], writes=[nr])
        sc.add("dve", lambda e: e.tensor_copy(out=hib, in_=crel), reads=["crel"], writes=["hib"])
        sc.add("dve", lambda e: e.tensor_copy(out=X[:, :, 0:8], in_=hib), reads=["hib"], writes=["X"])
        sc.add("dve", lambda e: e.tensor_tensor(out=X[:, :, 8:16], in0=crel, in1=X[:, :, 0:8], op=ALU.subtract),
               reads=["crel", "X"], writes=["X"])

    def cchain_b2(G):
        b = pbank()
        for t in range(4):
            sc.add("pe", lambda e, t=t, b=b: e.transpose(k.bank(b)[0:16, t * 128:(t + 1) * 128], X[:, t, :], k.identf),
                   reads=["X", "cf"], writes=["bank%d" % b])
        sc.add("dve", lambda e, b=b: e.tensor_copy(out=dq[0:16, :], in_=k.bank(b)[0:16, :]),
               reads=["bank%d" % b], writes=["dq"])


    def projections(G):
        cchain_a(G)
        for p in range(4):
            proj_fm(p * 128, lambda ps, br, p=p: sc.add(
                "act", lambda e: e.activation(out=qT2[:, p, :], in_=ps, func=AF.Copy, scale=0.125),
                reads=[br], writes=["qT2"]))
        cchain_a2(G)
        for p in range(4):
            proj_fm(512 + p * 128, lambda ps, br, p=p: sc.add(
                "dve", lambda e: e.tensor_copy(out=kT2[:, p, G * 512:(G + 1) * 512], in_=ps),
                reads=[br], writes=["kT2"]))
        cchain_b(G)
        for t in range(4):
            b = pbank()
            ps = k.bank(b)
            for kc in range(8):
                sc.add("pe", lambda e, kc=kc, t=t, ps=ps: e.matmul(
                    ps, hT[:, kc, t * 128:(t + 1) * 128], win[:, kc, 1024:1536], start=(kc == 0), stop=(kc == 7)),
                    reads=["win0", "hT"], writes=["bank%d" % b])
            sc.add("act", lambda e, t=t, ps=ps: e.activation(
                out=vaug[:, 4 * G + t, :, 0:64], in_=ps.rearrange("p (h d) -> p h d", h=8), func=AF.Copy),
                reads=["bank%d" % b], writes=["vaug"])
        cchain_b2(G)
        if G > 0:
            sc.add("dve", lambda e: e.tensor_copy(out=uT[:, :, 0:16], in_=uT[:, :, 512:528]),
                   reads=["uT"], writes=["uT"])
        for g in range(4):
            proj_fm(1544 + g * 128, lambda ps, br, g=g: sc.add(
                "act", lambda e: e.activation(out=uT[:, g, 16:528], in_=ps, func=AF.Copy),
                reads=[br], writes=["uT"]))

    def pool_mixer(G):
        for g in range(4):
            src = uT[:, g, :]
            L = g + 1
            bufs = [pA, pB]
            names = ["pA", "pB"]
            prev, prev_r = src, "uT"
            lo = 0
            for lev in range(L):
                sh = 1 << lev
                lo2 = lo + sh
                dst, dst_r = bufs[lev % 2], names[lev % 2]
                sc.add("dve", lambda e, dst=dst, prev=prev, lo2=lo2, sh=sh: e.tensor_tensor(
                    out=dst[:, lo2:528], in0=prev[:, lo2:528], in1=prev[:, lo2 - sh:528 - sh], op=ALU.add),
                    reads=[prev_r], writes=[dst_r])
                prev, prev_r, lo = dst, dst_r, lo2
            w = 1 << L
            sc.add("dve", lambda e, g=g, prev=prev, src=src, w=w: e.scalar_tensor_tensor(
                out=poolT[:, g, :], in0=prev[:, 16:528], scalar=1.0 / w, in1=src[:, 16:528],
                op0=ALU.mult, op1=ALU.subtract),
                reads=[prev_r, "uT"], writes=["poolT"])
            if G == 0:
                sc.add("dve", lambda e, prev=prev, w=w: e.tensor_tensor(
                    out=tmpc[:, 0:w - 1], in0=prev[:, 16:16 + w - 1], in1=k.invc[:, 0:w - 1], op=ALU.mult),
                    reads=[prev_r, "cf"], writes=["tmpc"])
                sc.add("dve", lambda e, g=g, src=src, w=w: e.tensor_tensor(
                    out=poolT[:, g, 0:w - 1], in0=tmpc[:, 0:w - 1], in1=src[:, 16:16 + w - 1], op=ALU.subtract),
                    reads=["tmpc", "uT"], writes=["poolT"])
            b = 0
            ps = k.bank(b)
            sc.add("pe", lambda e, g=g, ps=ps: e.matmul(ps, poolw[:, g, :], poolT[:, g, :], start=True, stop=True),
                   reads=["poolw", "poolT"], writes=["bank%d" % b])
            sc.add("act", lambda e, g=g, ps=ps: e.activation(out=catT[:, 4 + g, :], in_=ps, func=AF.Copy,
                                                            scale=pscale[:, g:g + 1]),
                   reads=["bank%d" % b, "pscale"], writes=["catT"])

    def attention(G, bg):
        nkt = 4 * G + 4
        per_it = -(-len(bg) // max(1, 4 * nkt - 3))
        items = [(p, kt) for p in range(4) for kt in range(nkt)]

        def geom(idx):
            p, kt = items[idx]
            j = kt - 4 * G
            sbs = (1 + 2 * (idx % 2), 2 + 2 * (idx % 2))
            pts = (2 * (idx % 2), 2 * (idx % 2) + 1)
            return p, kt, j, 128 * max(j, 0), sbs, pts

        def stageA(idx):
            p, kt, j, c0, sbs, pts = geom(idx)
            for hb_ in range(2):
                pb = 64 * hb_
                psS = k.bank(sbs[hb_])
                sc.add("pe", lambda e, pb=pb, psS=psS: e.matmul(
                    psS[:, c0:512], kT2[pb:pb + 64, p, kt * 128:(kt + 1) * 128], qT2[pb:pb + 64, p, c0:512],
                    start=True, stop=False, tile_position=(pb, 0)),
                    reads=["kT2", "qT2"], writes=["bank%d" % sbs[hb_]])
            for hb_ in range(2):
                h = 2 * p + hb_
                psS = k.bank(sbs[hb_])
                sc.add("pe", lambda e, h=h, psS=psS: e.matmul(
                    psS[:, c0:512], k.sel[:, h, :], dq[:, c0:512], start=False, stop=True),
                    reads=["cf", "dq"], writes=["bank%d" % sbs[hb_]])
            if j >= 0:
                for hb_ in range(2):
                    psS = k.bank(sbs[hb_])
                    sc.add("dve", lambda e, psS=psS: e.tensor_tensor(
                        out=psS[:, c0:c0 + 128], in0=psS[:, c0:c0 + 128], in1=k.maskneg, op=ALU.add),
                        reads=["bank%d" % sbs[hb_], "cf"], writes=["bank%d" % sbs[hb_]])

        def stageB(idx):
            p, kt, j, c0, sbs, pts = geom(idx)
            for hb_ in range(2):
                h = 2 * p + hb_
                psS = k.bank(sbs[hb_])
                pt = pT[pts[hb_]]
                sc.add("act", lambda e, h=h, psS=psS, pt=pt: e.activation(
                    out=pt[:, c0:512], in_=psS[:, c0:512], func=AF.Exp, bias=biasG[:, kt, h:h + 1]),
                    reads=["bank%d" % sbs[hb_], "biasG"], writes=["pT%d" % pts[hb_]])

        def stageC(idx):
            p, kt, j, c0, sbs, pts = geom(idx)
            for hb_ in range(2):
                h = 2 * p + hb_
                psO = k.bank(5 + hb_)
                pt = pT[pts[hb_]]
                sc.add("pe", lambda e, h=h, psO=psO, pt=pt: e.matmul(
                    psO[0:65, c0:512], vaug[:, kt, h, :], pt[:, c0:512], start=(kt == 0), stop=(kt == nkt - 1)),
                    reads=["vaug", "pT%d" % pts[hb_]], writes=["bank%d" % (5 + hb_)])

        def use_act(p):
            return G <= 2 or p == 3

        def epi1(p):
            for hb_ in range(2):
                psO = k.bank(5 + hb_)
                sc.add("dve", lambda e, hb_=hb_, psO=psO: e.tensor_copy(out=oun[hb_][0:65, :], in_=psO[0:65, :]),
                       reads=["bank%d" % (5 + hb_)], writes=["oun%d" % hb_])
                if use_act(p):
                    sc.add("act", lambda e, hb_=hb_: e.activation(out=oun[hb_][64:65, :], in_=oun[hb_][64:65, :],
                                                                  func=AF.Ln),
                           reads=["oun%d" % hb_], writes=["oun%d" % hb_])
                else:
                    sc.add("dve", lambda e, hb_=hb_: e.reciprocal(out=oun[hb_][64:65, :], in_=oun[hb_][64:65, :]),
                           reads=["oun%d" % hb_], writes=["oun%d" % hb_])

        def epi2(p):
            for hb_ in range(2):
                pb = 64 * hb_
                sc.add("pe", lambda e, hb_=hb_: e.matmul(k.bank(7)[0:64, :], k.onesf[64:65, 0:64],
                                                          oun[hb_][64:65, :], start=True, stop=True),
                       reads=["oun%d" % hb_, "cf"], writes=["b7fg", "b7w", "b7r"])
                if use_act(p):
                    sc.add("act", lambda e: e.activation(out=k.bank(7)[0:64, :], in_=k.bank(7)[0:64, :],
                                                         func=AF.Exp, scale=-1.0),
                           reads=["b7fg", "b7w", "b7r"], writes=["b7fg", "b7w", "b7r"])
                sc.add("dve", lambda e, hb_=hb_, pb=pb: e.tensor_tensor(
                    out=catT[pb:pb + 64, p, :], in0=oun[hb_][0:64, :], in1=k.bank(7)[0:64, :], op=ALU.mult),
                    reads=["oun%d" % hb_, "b7fg", "b7w", "b7r"], writes=["catT"])

        pending = []
        stageA(0)
        for idx in range(len(items)):
            if idx + 1 < len(items):
                stageA(idx + 1)
            stageB(idx)
            stageC(idx)
            sc.replay(bg, per_it)
            for ep in pending:
                ep[1] -= 1
            while pending and pending[0][1] <= 0:
                epi2(pending.pop(0)[0])
            p, kt = items[idx]
            if kt == nkt - 1:
                epi1(p)
                pending.append([p, 3])
        while pending:
            epi2(pending.pop(0)[0])
        sc.replay(bg)

    def outproj(G):
        for t in range(4):
            i = 4 * G + t
            b0 = 1 + 2 * (t % 2)
            psD = k.bank2(b0)
            for half in range(2):
                for kc in range(8):
                    sc.add("pe", lambda e, kc=kc, half=half, t=t, psD=psD: e.matmul(
                        psD[:, half * 512:(half + 1) * 512], catT[:, kc, t * 128:(t + 1) * 128],
                        wout[:, kc, half * 512:(half + 1) * 512], start=(kc == 0), stop=(kc == 7)),
                        reads=["catT", "wout%d" % half], writes=["bank%d" % (b0 + half)])
            post_norm_residual(k, i, psD, ["bank%d" % b0, "bank%d" % (b0 + 1)], gpost, "gpost")

    normA(0)
    normB(0)
    for G in range(NCH):
        projections(G)

        def background(G=G):
            pool_mixer(G)
            if G + 1 < NCH:
                normA(G + 1)
                normB(G + 1)
        attention(G, sc.capture(background))
        outproj(G)
    sc.fence()
    k.hb = hb_save
    ar.reset(m0)


def ffn_phase(k, l):
    sc = k.sc
    ar = k.arena
    m0 = ar.mark()
    wup, _ = ar.alloc_at(WUP_OFF, [8, DFF], BF16)
    wdn = ar.alloc([32, D], BF16)
    gpre = ar.alloc([D], F32)
    gpost = ar.alloc([D], F32)
    hT = ar.alloc([8, 512], BF16)
    aT = ar.alloc([32, 512], BF16)
    r = [ar.alloc([512], BF16) for _ in range(2)]
    assert ar.off <= WUP_OFF, ("ffn phase collides with w_up home", ar.off, WUP_OFF)
    if getattr(k, "wup_loaded", None) != l:
        load_w_cols(k, wup, k.w_up[l], "wup", [(j * 512, (j + 1) * 512) for j in range(8)])
    for j in range(8):
        sc.add("pool", lambda e, j=j: e.dma_start(
            out=wdn[:, 4 * j:4 * j + 4, :],
            in_=k.w_down[l, 512 * j:512 * (j + 1), :].rearrange("(f p) n -> p f n", p=128)),
            writes=["wdn%d" % j], dsem="wdn%d" % j)
    load_bcast(k, gpre, k.g_ffn_pre[l:l + 1, :], "gpre", "gpre")
    load_bcast(k, gpost, k.g_ffn_post[l:l + 1, :], "gpost", "gpost")

    def normT(c):
        pend = None
        for t in range(4):
            cur = (t,) + norm_tile(k, 4 * c + t, gpre, "gpre")
            if pend is not None:
                transpose_tile(k, 4 * c + pend[0], pend[1], pend[2], hT, "hT", pend[0] * 128, (0, 1))
            pend = cur
        transpose_tile(k, 4 * c + pend[0], pend[1], pend[2], hT, "hT", pend[0] * 128, (0, 1))

    def up(c):
        for f in range(32):
            ps = k.bank(2 + f % 2)
            pr = "bank%d" % (2 + f % 2)
            for kc in range(8):
                sc.add("pe", lambda e, kc=kc, f=f, ps=ps: e.matmul(
                    ps, wup[:, kc, f * 128:(f + 1) * 128], hT[:, kc, :], start=(kc == 0), stop=(kc == 7)),
                    reads=["wup%d" % (f // 4), "hT"], writes=[pr])
            rr = r[f % 2]
            rres = "r%d" % (f % 2)
            sc.add("act", lambda e, ps=ps, rr=rr: e.activation(out=rr, in_=ps, func=AF.Relu),
                   reads=[pr], writes=[rres])
            sc.add("dve", lambda e, f=f, rr=rr: e.tensor_tensor(out=aT[:, f, :], in0=rr, in1=rr, op=ALU.mult),
                   reads=[rres], writes=["aT"])

    def down(c, bg):
        per = -(-len(bg) // 200)
        for t in range(4):
            i = 4 * c + t
            b0 = 4 + 2 * (i % 2)
            psD = k.bank2(b0)
            for half in range(2):
                for f in range(32):
                    sc.add("pe", lambda e, f=f, half=half, psD=psD, t=t: e.matmul(
                        psD[:, half * 512:(half + 1) * 512], aT[:, f, t * 128:(t + 1) * 128],
                        wdn[:, f, half * 512:(half + 1) * 512], start=(f == 0), stop=(f == 31)),
                        reads=["aT", "wdn%d" % (f // 4)], writes=["bank%d" % (b0 + half)])
                    sc.replay(bg, per)
            post_norm_residual(k, i, psD, ["bank%d" % b0, "bank%d" % (b0 + 1)], gpost, "gpost")
        sc.replay(bg)

    normT(0)
    for c in range(NCH):
        up(c)
        down(c, sc.capture(lambda c=c: normT(c + 1)) if c + 1 < NCH else [])
    sc.fence()
    ar.reset(m0)


def build_program(phases=("mix", "cross", "ffn"), layers=range(DEPTH)):
    nc = bass.Bass("TRN2", target_bir_lowering=False)
    k = K()
    k.nc = nc

    def din(name, shape):
        return nc.dram_tensor(name, list(shape), F32, kind="ExternalInput").ap()

    k.x = din("x", [S, D])
    k.mem = din("mem", [MEM, D])
    k.g_mix_pre = din("g_mix_pre", [DEPTH, D])
    k.w_in = din("w_in", [DEPTH, D, IN_COLS])
    k.b_forget = din("b_forget", [DEPTH, 8])
    k.pool_w = din("pool_w", [DEPTH, 4, 128, 128])
    k.pool_scale = din("pool_scale", [DEPTH, 512])
    k.w_out = din("w_out", [DEPTH, D, D])
    k.g_mix_post = din("g_mix_post", [DEPTH, D])
    k.g_x_pre = din("g_x_pre", [DEPTH, D])
    k.g_mem = din("g_mem", [DEPTH, D])
    k.wq_x = din("wq_x", [DEPTH, D, D])
    k.wkv_x = din("wkv_x", [DEPTH, D, 2 * D])
    k.wo_x = din("wo_x", [DEPTH, D, D])
    k.g_x_post = din("g_x_post", [DEPTH, D])
    k.g_ffn_pre = din("g_ffn_pre", [DEPTH, D])
    k.w_up = din("w_up", [DEPTH, D, DFF])
    k.w_down = din("w_down", [DEPTH, DFF, D])
    k.g_ffn_post = din("g_ffn_post", [DEPTH, D])
    k.consts = din("consts", [128, NCONST])
    k.pool_scale_t = din("pool_scale_t", [DEPTH, 128, 4])
    k.xs = nc.dram_tensor("out", [S, D], F32, kind="ExternalOutput").ap()

    sc = Sched(nc)
    k.sc = sc
    k.prefetch_wup = ("cross" in phases and "ffn" in phases)
    k.wup_loaded = None
    ar = Arena(nc, 206 * 1024)
    k.arena = ar
    k.ident = ar.alloc([128], BF16)
    k.ones = ar.alloc([128], BF16)
    k.onecol = ar.alloc([1], F32)
    k.epsb = ar.alloc([1], F32)
    k.stat = ar.alloc([4, 4], F32)
    k.xin = [ar.alloc([D], F32) for _ in range(2)]
    k.hb = [ar.alloc([D], BF16) for _ in range(2)]
    k.yt = [ar.alloc([D], F32) for _ in range(2)]
    k.ps = nc.alloc_psum_tensor("ps", [128, 8 * 512], F32)
    k.bank = lambda b: k.ps[:, b * 512:(b + 1) * 512]
    k.bank2 = lambda b: k.ps[:, b * 512:(b + 2) * 512]
    k.bankT = lambda b: k.ps[:, b * 512:(b + 1) * 512].bitcast(BF16).rearrange("p (a b) -> p a b", a=8)

    sc.add("pool", lambda e: e.dma_start(out=k.ident, in_=k.consts[:, NCONST - 1024 - 128:NCONST - 1024]),
           writes=["ident"], dsem="ident")
    sc.add("dve", lambda e: e.memset(k.epsb, EPS), writes=["epsb"])
    sc.add("dve", lambda e: e.memset(k.ones, 1.0), writes=["ones"])
    sc.add("dve", lambda e: e.memset(k.onecol, 1.0), writes=["onecol"])

    for c in range(NCH):
        sc.add("sp", lambda e, c=c: e.dma_start(out=k.xs[c * 512:(c + 1) * 512, :], in_=k.x[c * 512:(c + 1) * 512, :]),
               writes=[("xs", 4 * c + t) for t in range(4)], dsem="xcopy%d" % c)
    for l in layers:
        if "mix" in phases:
            mix_phase(k, l)
        if "cross" in phases:
            cross_phase(k, l)
        if "ffn" in phases:
            ffn_phase(k, l)
    final = sorted(set(o.dsem for o in sc.ops if o.isdma), key=str)
    sc.emit(final_dsems=final)
    return nc


INPUT_NAMES = ["x", "mem", "g_mix_pre", "w_in", "b_forget", "pool_w", "pool_scale", "w_out", "g_mix_post",
               "g_x_pre", "g_mem", "wq_x", "wkv_x", "wo_x", "g_x_post", "g_ffn_pre", "w_up", "w_down",
               "g_ffn_post"]


def make_consts():
    c = np.zeros((128, NCONST), np.float32)
    eye = np.eye(128, dtype=np.float32)
    idx = np.arange(128)
    c[:, 0:128] = eye
    c[:, 128:256] = -(idx[:, None] <= idx[None, :]).astype(np.float32)
    c[127, 256:384] = 1.0
    c[:, 384:512] = np.where(idx[:, None] <= idx[None, :], 0.0, -30000.0)
    c[:, 512:528] = 1.0 / (np.arange(16) + 1.0)[None, :]
    c[:, 528:592] = 1.0
    c[:, 592:720] = eye
    selbase = 720
    for h in range(8):
        c[h, selbase + h * 128:selbase + (h + 1) * 128] = 1.0
        c[8 + h, selbase + h * 128:selbase + (h + 1) * 128] = 1.0
    return c


def kernel(**inputs):
    nc = build_program()
    consts = make_consts()
    in_maps = []
    for b in range(N_CORES):
        m = {}
        for n in INPUT_NAMES:
            a = np.asarray(inputs[n], dtype=np.float32)
            if n in ("x", "mem"):
                a = a[b]
            m[n] = np.ascontiguousarray(a)
        m["consts"] = consts
        m["pool_scale_t"] = np.ascontiguousarray(
            np.asarray(inputs["pool_scale"], dtype=np.float32).reshape(DEPTH, 4, 128).transpose(0, 2, 1))
        in_maps.append(m)
    res = run_bass_kernel_spmd(nc, in_maps, core_ids=list(range(N_CORES)))
    return np.stack([np.asarray(r["out"], dtype=np.float32) for r in res.results], axis=0)
```
